# Optimizing a Trainium2 kernel written in Bass

```python
import jax, jax.numpy as jnp
from jax import lax
import numpy as np

D_MODEL = 1024
BATCH = 8
SEQ = 2048
DEPTH = 1
DEC_BATCH = 16
DEC_SEQ = 64
PAST_LEN = 4096

CHUNK = 64
D_MIX = D_MODEL
D_A = D_MIX // 2
D_B = D_MIX - D_A
A_HEADS = 8
A_HEAD_DIM = D_A // A_HEADS
B_HEADS = 4
B_HEAD_DIM = D_B // B_HEADS
GMLP_CHUNK = 128
LRU_CONV = 4
LRU_C = 8.0
D_FF = 3 * D_MODEL
FFN_CONV = 3
D_IN = 2 * D_A + 2 * D_B
EPS = 1e-6

kernel_name = "hymba_rglru_gmlp_convffn_stream_step"


def rmsnorm(x, g):
    xf = x.astype(jnp.float32)
    y = xf * lax.rsqrt(jnp.mean(xf * xf, axis=-1, keepdims=True) + EPS) * g.astype(jnp.float32)
    return y.astype(x.dtype)


def layernorm(x, g, b):
    xf = x.astype(jnp.float32)
    mu = jnp.mean(xf, axis=-1, keepdims=True)
    var = jnp.mean(jnp.square(xf - mu), axis=-1, keepdims=True)
    y = (xf - mu) * lax.rsqrt(var + EPS) * g.astype(jnp.float32) + b.astype(jnp.float32)
    return y.astype(x.dtype)


def causal_dwconv(x, prev, w, b):
    k = w.shape[0]
    t = x.shape[1]
    xp = jnp.concatenate([prev.astype(x.dtype), x], axis=1)
    y = b
    for j in range(k):
        y = y + xp[:, j:j + t] * w[j]
    return y.astype(x.dtype), xp[:, xp.shape[1] - (k - 1):]


def rg_lru(x, h0, w_r, b_r, w_i, b_i, lam, reset_first):
    bsz, t, _ = x.shape
    xf = x.astype(jnp.float32)
    xh = xf.reshape(bsz, t, A_HEADS, A_HEAD_DIM)
    r = jax.nn.sigmoid(jnp.einsum('bthd,hde->bthe', xh, w_r.astype(jnp.float32)).reshape(bsz, t, D_A) + b_r)
    i = jax.nn.sigmoid(jnp.einsum('bthd,hde->bthe', xh, w_i.astype(jnp.float32)).reshape(bsz, t, D_A) + b_i)
    log_a = -LRU_C * r * jax.nn.softplus(-lam.astype(jnp.float32))
    a = jnp.exp(log_a)
    mult = jnp.sqrt(-jnp.expm1(2.0 * log_a))
    if reset_first:
        mult = mult.at[:, 0].set(1.0)
    bterm = mult * (i * xf)
    bterm = bterm.at[:, 0].add(a[:, 0] * h0.astype(jnp.float32))

    def combine(c1, c2):
        a1, b1 = c1
        a2, b2 = c2
        return a1 * a2, a2 * b1 + b2

    _, h = lax.associative_scan(combine, (a, bterm), axis=1)
    return h.astype(x.dtype), h[:, -1].astype(x.dtype)


def spatial_gate(u, vn, w_s, b_s):
    bsz, t, _ = vn.shape
    l = min(t, GMLP_CHUNK)
    n = t // l
    vh = vn.reshape(bsz, n, l, B_HEADS, B_HEAD_DIM)
    blk = jnp.arange(l) // CHUNK
    mask = blk[:, None] >= blk[None, :]
    w = jnp.where(mask[None], w_s[:, :l, :l], 0.0)
    s = jnp.einsum('hij,bcjhd->bcihd', w, vh) + b_s[:, :l].T[None, None, :, :, None]
    return (u * s.reshape(bsz, t, D_B)).astype(u.dtype)


def layer(x, h0, conv_prev, ffn_prev, reset_first,
          g_pre1, w_in, w_conv_a, b_conv_a, w_r, b_r, w_i, b_i, lam, g_out_a,
          g_v, b_v, w_s, b_s, g_out_b, w_o, g_post1,
          g_pre2, w_up, w_conv_f, b_conv_f, w_down, g_post2):
    hn = rmsnorm(x, g_pre1)
    z = hn @ w_in
    gate_a = z[..., :D_A]
    xa = z[..., D_A:2 * D_A]
    u = z[..., 2 * D_A:2 * D_A + D_B]
    v = z[..., 2 * D_A + D_B:]
    xa_c, new_conv = causal_dwconv(xa, conv_prev, w_conv_a, b_conv_a)
    h, h_last = rg_lru(xa_c, h0, w_r, b_r, w_i, b_i, lam, reset_first)
    ya = rmsnorm(h * jax.nn.gelu(gate_a), g_out_a)
    vn = layernorm(v, g_v, b_v)
    yb = rmsnorm(spatial_gate(u, vn, w_s, b_s), g_out_b)
    mix = jnp.concatenate([ya, yb], axis=-1) @ w_o
    x = x + rmsnorm(mix, g_post1)
    hn2 = rmsnorm(x, g_pre2)
    up = hn2 @ w_up
    upc, new_ffn = causal_dwconv(up, ffn_prev, w_conv_f, b_conv_f)
    f = (jax.nn.gelu(upc[..., :D_FF]) * upc[..., D_FF:]) @ w_down
    x = x + rmsnorm(f, g_post2)
    return x, h_last, new_conv, new_ffn, vn


def setup_inputs(seed: int = 0) -> dict:
    key = jax.random.key(seed)
    ks = iter(jax.random.split(key, 40))
    nrm = lambda shape, s: jax.random.normal(next(ks), shape, jnp.float32) * s
    gain = lambda shape: 1.0 + nrm(shape, 0.05)
    a_base = jax.random.uniform(next(ks), (DEPTH, D_A), jnp.float32, 0.9, 0.999)
    a_root = a_base ** (1.0 / LRU_C)
    lam = jnp.log(a_root) - jnp.log1p(-a_root)
    return {
        "x_prompt": nrm((BATCH, SEQ, D_MODEL), 1.0),
        "x_sample": nrm((DEC_BATCH, DEC_SEQ, D_MODEL), 1.0),
        "state_lru_h": nrm((DEPTH, DEC_BATCH, D_A), 0.5),
        "state_lru_conv": nrm((DEPTH, DEC_BATCH, LRU_CONV - 1, D_A), 1.0),
        "state_ffn_conv": nrm((DEPTH, DEC_BATCH, FFN_CONV - 1, 2 * D_FF), 1.0),
        "g_pre1": gain((DEPTH, D_MODEL)),
        "w_in": nrm((DEPTH, D_MODEL, D_IN), D_MODEL ** -0.5),
        "w_conv_a": nrm((DEPTH, LRU_CONV, D_A), LRU_CONV ** -0.5),
        "b_conv_a": nrm((DEPTH, D_A), 0.02),
        "w_r": nrm((DEPTH, A_HEADS, A_HEAD_DIM, A_HEAD_DIM), A_HEAD_DIM ** -0.5),
        "b_r": nrm((DEPTH, D_A), 0.02),
        "w_i": nrm((DEPTH, A_HEADS, A_HEAD_DIM, A_HEAD_DIM), A_HEAD_DIM ** -0.5),
        "b_i": nrm((DEPTH, D_A), 0.02),
        "lam": lam,
        "g_out_a": gain((DEPTH, D_A)),
        "g_v": gain((DEPTH, D_B)),
        "b_v": nrm((DEPTH, D_B), 0.02),
        "w_s": nrm((DEPTH, B_HEADS, GMLP_CHUNK, GMLP_CHUNK), GMLP_CHUNK ** -0.5),
        "b_s": gain((DEPTH, B_HEADS, GMLP_CHUNK)),
        "g_out_b": gain((DEPTH, D_B)),
        "w_o": nrm((DEPTH, D_MIX, D_MODEL), D_MIX ** -0.5),
        "g_post1": gain((DEPTH, D_MODEL)),
        "g_pre2": gain((DEPTH, D_MODEL)),
        "w_up": nrm((DEPTH, D_MODEL, 2 * D_FF), D_MODEL ** -0.5),
        "w_conv_f": nrm((DEPTH, FFN_CONV, 2 * D_FF), FFN_CONV ** -0.5),
        "b_conv_f": nrm((DEPTH, 2 * D_FF), 0.02),
        "w_down": nrm((DEPTH, D_FF, D_MODEL), D_FF ** -0.5),
        "g_post2": gain((DEPTH, D_MODEL)),
    }


def reference(x_prompt, x_sample, state_lru_h, state_lru_conv, state_ffn_conv,
              g_pre1, w_in, w_conv_a, b_conv_a, w_r, b_r, w_i, b_i, lam, g_out_a,
              g_v, b_v, w_s, b_s, g_out_b, w_o, g_post1,
              g_pre2, w_up, w_conv_f, b_conv_f, w_down, g_post2):
    bp = x_prompt.shape[0]
    xp, xs = x_prompt, x_sample
    hp_l, cp_l, fp_l, hs_l, cs_l, fs_l, vs_l = [], [], [], [], [], [], []
    for l in range(DEPTH):
        w = (g_pre1[l], w_in[l], w_conv_a[l], b_conv_a[l], w_r[l], b_r[l], w_i[l], b_i[l], lam[l],
             g_out_a[l], g_v[l], b_v[l], w_s[l], b_s[l], g_out_b[l], w_o[l], g_post1[l],
             g_pre2[l], w_up[l], w_conv_f[l], b_conv_f[l], w_down[l], g_post2[l])
        h0_p = jnp.zeros((bp, D_A), xp.dtype)
        conv0_p = jnp.zeros((bp, LRU_CONV - 1, D_A), xp.dtype)
        ffn0_p = jnp.zeros((bp, FFN_CONV - 1, 2 * D_FF), xp.dtype)
        xp, hp, cp, fp, _ = layer(xp, h0_p, conv0_p, ffn0_p, True, *w)
        xs, hs, cs, fs, vs = layer(xs, state_lru_h[l], state_lru_conv[l], state_ffn_conv[l], False, *w)
        hp_l.append(hp); cp_l.append(cp); fp_l.append(fp)
        hs_l.append(hs); cs_l.append(cs); fs_l.append(fs); vs_l.append(vs)
    return (xp, xs,
            jnp.stack(hp_l), jnp.stack(cp_l), jnp.stack(fp_l),
            jnp.stack(hs_l), jnp.stack(cs_l), jnp.stack(fs_l), jnp.stack(vs_l))
```

```python
import numpy as np
from contextlib import ExitStack
import concourse.bass as bass
import concourse.mybir as mybir
from concourse.bass_utils import run_bass_kernel_spmd

F32 = mybir.dt.float32
BF16 = mybir.dt.bfloat16
AF = mybir.ActivationFunctionType
ALU = mybir.AluOpType
AX = mybir.AxisListType

NCORES = 8
import os as _os
_STOP = _os.environ.get("KSTOP", "")
D = 1024
DA = 512
DFF = 3072
EPS = 1e-6
NPH = 1024
NSH = 64
NTT = 9
HCOLS = 2 + NPH + NSH
GCOLS = NPH + 2 + NSH

_off = {}
_k = 0
for _n, _w in [("gpre1", 8), ("gpre2", 8), ("wca", 16), ("bca", 4), ("br", 4), ("bi", 4), ("lam", 4),
               ("goa", 4), ("gob", 4), ("gv", 4), ("wcf", 144), ("bcf", 48), ("h0", 8), ("lconv", 24), ("ffnst", 192)]:
    _off[_n] = _k
    _k += _w
KCST = _k
BC_GV, BC_BV, BC_GP1, BC_GP2, KBC = 0, 512, 1024, 2048, 3072
OS_H, OS_CONV, OS_FFN, KOSM = 0, 12, 48, 48 + 288


class Sched:
    def __init__(self, nc, es):
        self.nc = nc
        self.es = es
        self.eng = {"pe": nc.tensor, "dve": nc.vector, "act": nc.scalar, "pool": nc.gpsimd, "sp": nc.sync}
        self.sem = {}
        self.cnt = {}
        for e in ("pe", "dve", "act", "pool"):
            self.sem[e] = es.enter_context(nc.semaphore("c_" + e))
            self.cnt[e] = 0
        self.dsem = {}
        self.dcnt = {}
        self.waited = {}
        self.buf = {}
        self.nsem = 0
        self.defer = None
        self.eng_free = {"pe": 0.0, "dve": 0.0, "act": 0.0, "pool": 0.0, "sp": 0.0}
        self.act_set = None

    def _dma_sem(self, name):
        if name not in self.dsem:
            self.dsem[name] = self.es.enter_context(self.nc.semaphore("d_" + name))
            self.dcnt[name] = 0
        return self.dsem[name]

    def _b(self, k):
        if k not in self.buf:
            self.buf[k] = {"w": None, "r": {}}
        return self.buf[k]

    def _deps(self, r, w):
        deps = []
        for k in r:
            b = self._b(k)
            if b["w"] is not None:
                deps.append(b["w"])
        for k in w:
            b = self._b(k)
            if b["w"] is not None:
                deps.append(b["w"])
            deps.extend(b["r"].values())
        return deps

    def _wait(self, e, deps):
        best = {}
        for (sid, sem, val, src) in deps:
            if src == "pe" and e == "pe":
                continue
            if self.waited.get((e, sid), 0) >= val:
                continue
            if best.get(sid, (None, 0))[1] < val:
                best[sid] = (sem, val)
        for sid, (sem, val) in best.items():
            self.eng[e].wait_ge(sem, val)
            self.waited[(e, sid)] = val

    def _record(self, tok, r, w):
        for k in r:
            b = self._b(k)
            old = b["r"].get(tok[0])
            if old is None or old[2] < tok[2]:
                b["r"][tok[0]] = tok
        for k in w:
            b = self._b(k)
            b["w"] = tok
            b["r"] = {}

    DEF_T = {"pe": 0.27, "dve": 0.6, "act": 0.8, "pool": 1.5, "sp": 0.1}

    ASETS = {"g": (11,), "t": (11, 0), "e": (0,), "s": (3,)}

    def op(self, e, fn, r=(), w=(), inc=True, t=None, aset=None):
        xb = [k for k in r if k[0] == "B" and k[1:].isdigit() and k not in w]
        if xb:
            w = tuple(w) + tuple(xb)
        if self.defer is not None:
            self.defer.append(("op", e, fn, tuple(r), tuple(w), inc, self.DEF_T[e] if t is None else t, aset))
            return None
        self._wait(e, self._deps(r, w))
        inst = fn(self.eng[e])
        if inc:
            self.cnt[e] += 1
            inst.then_inc(self.sem[e], 1)
            tok = ("c_" + e, self.sem[e], self.cnt[e], e)
        else:
            tok = ("c_" + e, self.sem[e], self.cnt[e] + 1, e)
        self._record(tok, r, w)
        return inst

    def dma(self, q, sname, out, in_, r=(), w=(), **kw):
        if self.defer is not None:
            self.defer.append(("dma", q, (sname, out, in_, kw), tuple(r), tuple(w), True, 0.1, None))
            return None
        sem = self._dma_sem(sname)
        self._wait(q, self._deps(r, w))
        inst = self.eng[q].dma_start(out=out, in_=in_, **kw)
        self.dcnt[sname] += 16
        inst.then_inc(sem, 16)
        tok = ("d_" + sname, sem, self.dcnt[sname], "dma")
        self._record(tok, r, w)
        return inst

    def fence(self, new_keys, old_keys):
        toks = {}
        for k in old_keys:
            b = self._b(k)
            cands = list(b["r"].values())
            if b["w"] is not None:
                cands.append(b["w"])
            for t in cands:
                if t[0] not in toks or toks[t[0]][2] < t[2]:
                    toks[t[0]] = t
        for k in new_keys:
            nb = self._b(k)
            for sid, t in toks.items():
                if sid not in nb["r"] or nb["r"][sid][2] < t[2]:
                    nb["r"][sid] = t

    def group_written(self, sname, keys):
        tok = ("d_" + sname, self.dsem[sname], self.dcnt[sname], "dma")
        for k in keys:
            b = self._b(k)
            b["w"] = tok
            b["r"] = {}

    def finish(self, e="sp"):
        for name, sem in self.dsem.items():
            if self.dcnt[name] > 0:
                self.eng[e].wait_ge(sem, self.dcnt[name])


def build_program():
    nc = bass.Bass("TRN2", target_bir_lowering=False)
    es = ExitStack()
    with es:
        _build(nc, es)
    return nc


def _build(nc, es):
    S = Sched(nc, es)

    def dram(name, shape, kind):
        return nc.dram_tensor(name, list(shape), F32, kind=kind).ap()

    x_d = dram("x", [17, 128, D], "ExternalInput")
    cst_d = dram("cst", [128, KCST], "ExternalInput")
    bc_d = dram("bc", [128, KBC], "ExternalInput")
    ident_d = dram("ident", [128, 128], "ExternalInput")
    win_d = dram("w_in", [128, 8, 2048], "ExternalInput")
    wo_d = dram("w_o", [128, 8, 1024], "ExternalInput")
    wup_d = dram("w_up", [24, 128, 2048], "ExternalInput")
    wdn_d = dram("w_down", [128, 24, 1024], "ExternalInput")
    wr_d = dram("w_r", [8, 64, 64], "ExternalInput")
    wi_d = dram("w_i", [8, 64, 64], "ExternalInput")
    wst_d = dram("w_sT", [4, 128, 128], "ExternalInput")
    bs_d = dram("b_s", [1, 4, 128], "ExternalInput")
    y_d = dram("y", [17, 128, D], "ExternalOutput")
    vns_d = dram("vns", [2, 64, 512], "ExternalOutput")
    osm_d = dram("osm", [128, KOSM], "ExternalOutput")

    def sb(name, shape, dt=F32):
        return es.enter_context(nc.sbuf_tensor("sb_" + name, list(shape), dt))

    def ps(name):
        return es.enter_context(nc.psum_tensor(name, [128, 512], F32))

    cst = sb("cst", [128, KCST])
    bc = sb("bc", [128, KBC])
    ident = sb("ident", [128, 128])
    wr_bd = sb("wr_bd", [128, 4, 128], BF16)
    wi_bd = sb("wi_bd", [128, 4, 128], BF16)
    wsP = sb("wsP", [128, 4, 128], BF16)
    wsS = sb("wsS", [64, 4, 64], BF16)
    bsr = sb("bsr", [1, 4, 128], BF16)
    ones_r = sb("ones_r", [1, 128], BF16)
    ones_c = sb("ones_c", [128, 1], BF16)
    igf = sb("igf", [1, 512])
    rowb = sb("rowb", [1, 2048], BF16)
    gvgob = sb("gvgob", [128, 4])
    clam = sb("clam", [128, 16])
    invg2 = sb("invg2", [128, 8], BF16)
    smt = sb("smt", [128, 16])
    osm = sb("osm", [128, KOSM])
    x1 = sb("x1", [128, NTT, D])
    hn2T = sb("hn2T", [128, 8, HCOLS], BF16)
    r48 = sb("r48", [128, 24 * 1024], BF16)
    ARENA = 77 * 1024
    arena = sb("arena", [128, ARENA // 4])

    class Carve:
        def __init__(self, off=0):
            self.off = off

        def take(self, shape, dt):
            esz = 4 if dt == F32 else 2
            n = int(np.prod(shape[1:]))
            nbytes = (n * esz + 31) // 32 * 32
            assert self.off + nbytes <= ARENA, ("arena overflow", self.off, nbytes)
            base = arena[:, self.off // 4:(self.off + nbytes) // 4]
            self.off += nbytes
            v = base.bitcast(dt) if dt != F32 else base
            v = v[:, 0:n]
            if len(shape) == 3:
                v = v.rearrange("p (a b) -> p a b", a=shape[1])
            return v

    ca = Carve()
    hnT = ca.take([128, 8, 512], BF16)
    u_sb = ca.take([128, 4, 512], F32)
    yT = ca.take([128, 8, 512], BF16)
    split = ca.off
    gg4 = ca.take([128, 4, 512], F32)
    xa4 = ca.take([128, 4, 520], F32)
    S2 = []
    for ci in range(2):
        S2.append(dict(xc=ca.take([128, 512], F32), xcb=ca.take([128, 512], BF16), rr=ca.take([128, 512], F32),
                       ii=ca.take([128, 512], F32), aa=ca.take([128, 512], F32), hbuf=ca.take([128, 512], F32),
                       sqa=ca.take([128, 512], BF16)))
    ct_ = Carve(split)
    TP = []
    for ci in range(3):
        TP.append(dict(xn=ct_.take([128, D], F32), junk=ct_.take([128, D], BF16), vn=ct_.take([128, 512], F32),
                       vnb=ct_.take([128, 512], BF16), tmpA=ct_.take([128, 512], F32), mix=ct_.take([128, D], F32),
                       sqb=ct_.take([128, 4, 128], BF16)))
    S2_KEYS = ["gg%d" % c for c in range(4)] + ["xa%d" % c for c in range(4)] + \
              [n + str(ci) for ci in range(4) for n in ("xc", "xcb", "rr", "ii", "aa", "hbuf", "sqa")]
    TP_KEYS = [n + str(ci) for ci in range(3) for n in ("xn", "junk", "vn", "vnb", "tmpA", "mix", "sqb")]
    SH_KEYS = ["hnT%d" % s for s in range(4)] + ["u_sb"] + ["yTa%d" % c for c in range(4)] + ["yTb%d" % c for c in range(4)]
    cb = Carve()
    wup = [cb.take([128, 8, 256], BF16) for _ in range(3)]
    gT = cb.take([128, 24, GCOLS], BF16)
    NSET = 3
    accg = [cb.take([128, 512], F32) for _ in range(NSET)]
    accl = [cb.take([128, 512], F32) for _ in range(NSET)]
    junk2 = hn2T[:, 0, 2:514]
    ARENA_KEYS_A = S2_KEYS + TP_KEYS + SH_KEYS
    ARENA_KEYS_B = ["wup0", "wup1", "wup2"] + ["accg%d" % i for i in range(NSET)] + \
                   ["accl%d" % i for i in range(NSET)]
    GT_KEYS = ["gT%d" % k for k in range(24)]
    WA_KEYS = ["win%d" % k for k in range(8)] + ["wo%d" % k for k in range(4)]
    WD_KEYS = ["wdn%d" % k for k in range(12)]
    HN2_KEYS = ["hn2T_%d" % t for t in range(NTT)] + ["hn2T_h"]

    win_sb = r48[:, 0:8 * 2048].rearrange("p (a b) -> p a b", a=8)
    wo_sb = r48[:, 8 * 2048:8 * 2048 + 8 * 1024].rearrange("p (a b) -> p a b", a=8)
    wdn_sb = r48[:, :].rearrange("p (a b) -> p a b", a=24)

    B = [ps("bank%d" % i) for i in range(8)]

    def C(name, width=1, idx=0):
        o = _off[name] + idx
        return cst[:, o:o + width]

    def run_chains(gens):
        gens = list(gens)
        while gens:
            for g in list(gens):
                try:
                    next(g)
                except StopIteration:
                    gens.remove(g)

    def run_window(makers, lag=0, window=2):
        run_dag([(m, [k - window] if k >= window else []) for k, m in enumerate(makers)])

    def run_dag(chains):
        LAT = 0.2
        TOL = 0.0
        n = len(chains)
        state = [None] * n
        ndone = 0

        def pull(ch):
            ch["pending"] = []
            S.defer = ch["pending"]
            try:
                next(ch["gen"])
            except StopIteration:
                ch["done"] = True
            S.defer = None

        while ndone < n:
            for k in range(n):
                if state[k] is None and all(state[a_] == "done" for a_ in chains[k][1]):
                    ch = {"gen": chains[k][0](), "ready": 0.0, "done": False, "pending": []}
                    pull(ch)
                    state[k] = ch
            best, bt, bk = None, None, None
            for k in range(n):
                ch = state[k]
                if not isinstance(ch, dict):
                    continue
                if ch["done"] and not ch["pending"]:
                    state[k] = "done"
                    ndone += 1
                    continue
                cand = max(ch["ready"], S.eng_free[ch["pending"][0][1]]) if ch["pending"] else ch["ready"]
                if ch["pending"]:
                    a0 = ch["pending"][0][7]
                    if a0 is not None and S.act_set not in S.ASETS[a0]:
                        cand += 1.3
                if bt is None or cand < bt - TOL:
                    best, bt, bk = ch, cand, k
            if best is None:
                continue
            for (kind, e, fn, r, w, inc, t, aset) in best["pending"]:
                st_ = max(best["ready"], S.eng_free[e])
                if aset is not None and S.act_set not in S.ASETS[aset]:
                    S.act_set = S.ASETS[aset][0]
                    st_ += 1.3
                S.eng_free[e] = st_ + t
                best["ready"] = st_ + t + LAT
                if kind == "op":
                    S.op(e, fn, r=r, w=w, inc=inc)
                else:
                    sname, out, in_, kw = fn
                    S.dma(e, sname, out, in_, r=r, w=w, **kw)
            best["pending"] = []
            if not best["done"]:
                pull(best)

    S.dma("sp", "cst", cst[:], cst_d[:, :], w=["cst"])
    S.dma("sp", "ident", ident[:], ident_d[:, :], w=["ident"])
    for tt in range(4):
        S.dma("sp", "xin%d" % tt, x1[:, tt, :], x_d[tt, :, :], w=["x1_%d" % tt])
    S.dma("sp", "bc", bc[:], bc_d[:, :], w=["bc"])
    S.op("dve", lambda e: e.memset(wr_bd[:], 0.0), w=["wr_bd"])
    S.op("dve", lambda e: e.memset(wi_bd[:], 0.0), w=["wi_bd"])
    S.op("dve", lambda e: e.memset(wsP[:], 0.0), w=["wsP"])
    S.op("dve", lambda e: e.memset(ones_r[:], 1.0), w=["ones_r"])
    S.op("dve", lambda e: e.memset(ones_c[:], 1.0), w=["ones_c"])
    S.op("dve", lambda e: e.memset(osm[:], 0.0), w=["osm"])
    S.op("dve", lambda e: e.memset(hn2T[:, :, 0:2], 0.0), w=["hn2T_h"])

    def small_weight_dmas():
        for hh in range(8):
            ct, o = hh // 2, (hh % 2) * 64
            S.dma("pool", "wsm", wr_bd[o:o + 64, ct, o:o + 64], wr_d[hh], r=["wr_bd"])
            S.dma("pool", "wsm", wi_bd[o:o + 64, ct, o:o + 64], wi_d[hh], r=["wi_bd"])
        S.dma("pool", "wsm", wsP[0:64, :, :], wst_d[:, 0:64, :].rearrange("h j i -> j h i"), r=["wsP"])
        S.dma("pool", "wsm", wsP[64:128, :, 64:128], wst_d[:, 64:128, 64:128].rearrange("h j i -> j h i"), r=["wsP"])
        S.dma("pool", "wsm", wsS[:, :, :], wst_d[:, 0:64, 0:64].rearrange("h j i -> j h i"))
        S.dma("pool", "wsm", bsr[:, :, :], bs_d[:, :, :])
        S.dma("pool", "wsm", rowb[0:1, 1536:2048], bs_d[0:1, :, :].rearrange("o h i -> o (h i)"))
        S.group_written("wsm", ["wr_bd", "wi_bd", "wsP", "wsS", "bsr", "rowb_bs"])

    S.op("act", lambda e: e.activation(out=smt[:, 0:4], in_=C("lam", 4), func=AF.Exp, scale=-1.0), r=["cst"], w=["smt"])
    S.op("act", lambda e: e.activation(out=smt[:, 4:8], in_=smt[:, 0:4], func=AF.Ln, bias=1.0), r=["smt"], w=["smt"])
    S.op("dve", lambda e: e.tensor_scalar(out=clam[:, 0:4], in0=smt[:, 4:8], scalar1=-4.0, scalar2=None, op0=ALU.mult),
         r=["smt"], w=["clam"])
    S.op("dve", lambda e: e.tensor_scalar(out=clam[:, 4:8], in0=smt[:, 4:8], scalar1=-8.0, scalar2=None, op0=ALU.mult),
         r=["smt"], w=["clam"])
    S.op("dve", lambda e: e.tensor_scalar(out=clam[:, 8:12], in0=C("br", 4), scalar1=0.5, scalar2=None, op0=ALU.mult),
         r=["cst"], w=["clam"])
    S.op("dve", lambda e: e.tensor_scalar(out=clam[:, 12:16], in0=C("bi", 4), scalar1=0.5, scalar2=None, op0=ALU.mult),
         r=["cst"], w=["clam"])
    S.op("dve", lambda e: e.tensor_tensor(out=smt[:, 8:12], in0=C("goa", 4), in1=C("goa", 4), op=ALU.mult),
         r=["cst"], w=["smt"])
    S.op("dve", lambda e: e.tensor_tensor(out=smt[:, 12:16], in0=C("gob", 4), in1=C("gob", 4), op=ALU.mult),
         r=["cst"], w=["smt"])
    S.op("dve", lambda e: e.reciprocal(out=smt[:, 0:8], in_=smt[:, 8:16]), r=["smt"], w=["smt"])
    S.op("dve", lambda e: e.tensor_copy(out=invg2[:, :], in_=smt[:, 0:8]), r=["smt"], w=["invg2"])
    S.op("dve", lambda e: e.tensor_tensor(out=gvgob[:, :], in0=C("gv", 4), in1=C("gob", 4), op=ALU.mult),
         r=["cst"], w=["gvgob"])
    S.op("dve", lambda e: e.reciprocal(out=igf[0:1, :], in_=bc[0:1, BC_GV:BC_GV + 512]), r=["bc"], w=["igf"])
    S.op("dve", lambda e: e.tensor_copy(out=rowb[0:1, 0:512], in_=igf[0:1, :]), r=["igf"], w=["rowb_g"])
    S.op("dve", lambda e: e.tensor_tensor(out=rowb[0:1, 512:1024], in0=bc[0:1, BC_BV:BC_BV + 512], in1=igf[0:1, :],
                                          op=ALU.mult), r=["bc", "igf"], w=["rowb_b"])

    def colsum_setup():
        for hd in range(4):
            S.op("pe", lambda e, hd=hd: e.matmul(B[7][0:1, hd * 128:(hd + 1) * 128], lhsT=ones_c[:, 0:1],
                                                rhs=wsP[:, hd, :], start=True, stop=True),
                 r=["ones_c", "wsP"], w=["B7"], inc=(hd == 3))
        S.op("dve", lambda e: e.tensor_copy(out=rowb[0:1, 1024:1536], in_=B[7][0:1, 0:512]), r=["B7"], w=["rowb_cs"])

    smc = [sb("sm%d" % ci, [128, 32]) for ci in range(3)]
    smg = sb("smg", [128, 32])
    SS2, RAB = smg[:, 0:8], smg[:, 8:16]

    def SM(ci, name):
        lay = {"ssq": (0, 1), "rs": (1, 2), "mv": (2, 4), "st6": (4, 10), "rv": (10, 11), "nmr": (11, 12),
               "ssm": (12, 13), "rm": (13, 14), "ssf": (14, 16), "rf": (16, 17)}
        a, b = lay[name]
        return smc[ci][:, a:b], name + str(ci)

    def rstd(dst, src, scale, rows, keys_r, key_w):
        S.op("act", lambda e: e.activation(out=dst[:rows], in_=src[:rows], func=AF.Sqrt, scale=scale, bias=EPS),
             r=keys_r, w=[key_w], t=0.25, aset="s")
        yield
        S.op("dve", lambda e: e.reciprocal(out=dst[:rows], in_=dst[:rows]), r=[key_w], w=[key_w], t=0.18)
        yield

    def ch_norm_transpose(tt, rows, ci, dstT, col0, gname, dkey):
        xt = x1[:rows, tt, :]
        xk = "x1_%d" % tt
        T = TP[ci]
        tb = (6, 7) if ci == 1 else (0, 1)
        ssq, kssq = SM(ci, "ssq")
        rs, krs = SM(ci, "rs")
        S.op("act", lambda e: e.activation(out=T["junk"][:rows, :], in_=xt, func=AF.Square, accum_out=ssq[:rows]),
             r=[xk], w=["junk%d" % ci, kssq])
        yield
        yield from rstd(rs, ssq, 1.0 / D, rows, [kssq], krs)
        S.op("act", lambda e: e.activation(out=T["xn"][:rows, :], in_=xt, func=AF.Copy, scale=rs[:rows]),
             r=[xk, krs], w=["xn%d" % ci], t=1.1)
        yield
        for kt in range(8):
            bk = B[tb[kt // 4]]
            c = (kt % 4) * 128
            S.op("pe", lambda e, kt=kt, bk=bk, c=c: e.transpose(out=bk[:, c:c + rows],
                                                                 in_=T["xn"][:rows, kt * 128:(kt + 1) * 128],
                                                                 identity=ident[:rows, :rows]),
                 r=["xn%d" % ci, "ident"], w=["B%d" % tb[kt // 4]], inc=(kt % 4 == 3))
        go = _off[gname]
        for hb in range(2):
            bk = B[tb[hb]]
            gsl = cst[:, go + 4 * hb:go + 4 * hb + 4].unsqueeze(2).to_broadcast([128, 4, 128])
            S.op("dve", lambda e, bk=bk, gsl=gsl, hb=hb: e.tensor_tensor(
                out=dstT[:, 4 * hb:4 * hb + 4, col0:col0 + rows],
                in0=bk[:, 0:512].rearrange("p (a b) -> p a b", a=4)[:, :, 0:rows],
                in1=gsl[:, :, 0:rows], op=ALU.mult),
                 r=["B%d" % tb[hb], "cst"], w=[dkey])
        yield

    def phase_A(h):
        for tt in range(NTT):
            if h == 0 and tt < 4:
                continue
            rows = 64 if tt == 8 else 128
            src = x_d[16, h * 64:(h + 1) * 64, :] if tt == 8 else x_d[h * 8 + tt, :, :]
            S.dma("sp", "xin%d" % tt, x1[:rows, tt, :], src, w=["x1_%d" % tt])
        for st in range(3):
            isS = (st == 2)
            N = NSH if isS else 512
            nsub = 1 if isS else 4
            rows = 64 if isS else 128
            seq = (1 + h) if isS else 0
            tts = [8] if isS else [st * 4 + s for s in range(nsub)]
            HNK = ["hnT%d" % s for s in range(nsub)]
            if st == 0:
                run_window([(lambda s=s: ch_norm_transpose(tts[s], rows, s % 2, hnT, s * 128, "gpre1", "hnT%d" % s))
                            for s in range(nsub)])
            S.fence(S2_KEYS, TP_KEYS)

            def chz(ct):
                zb = (2, 3, 4)
                for (bk, c0) in ((zb[0], ct * 128), (zb[1], 512 + ct * 128), (zb[2], 1024 + ct * 128)):
                    for kt in range(8):
                        S.op("pe", lambda e, bk=bk, c0=c0, kt=kt: e.matmul(B[bk][:, 0:N], lhsT=win_sb[:, kt, c0:c0 + 128],
                                                                          rhs=hnT[:, kt, 0:N], start=(kt == 0),
                                                                          stop=(kt == 7)),
                             r=["win%d" % kt] + HNK, w=["B%d" % bk], inc=(kt == 7))
                yield
                S.op("act", lambda e: e.activation(out=gg4[:, ct, 0:N], in_=B[zb[0]][:, 0:N], func=AF.Gelu_apprx_tanh),
                     r=["B%d" % zb[0]], w=["gg%d" % ct], aset="g")
                yield
                S.op("act", lambda e: e.activation(out=xa4[:, ct, 3:3 + N], in_=B[zb[1]][:, 0:N], func=AF.Copy),
                     r=["B%d" % zb[1]], w=["xa%d" % ct])
                yield
                S.op("act", lambda e: e.activation(out=u_sb[:, ct, 0:N], in_=B[zb[2]][:, 0:N], func=AF.Copy),
                     r=["B%d" % zb[2]], w=["u_sb"])
                yield

            def chain_bufs(ct, ci):
                if not isS:
                    return S2[ci], 0
                off = 128 * (ct // 2)
                return {k: v[:, off:off + 128] for k, v in S2[ct % 2].items()}, off

            def ch2a(ct, ci):
                Q, boff = chain_bufs(ct, ci)
                zb = (5, 6) if ct % 2 == 0 else (7, 1)
                bR, bI = B[zb[0]][:, boff:], B[zb[1]][:, boff:]
                xa = xa4[:, ct, :]
                kxa = "xa%d" % ct
                if isS:
                    o = _off["lconv"] + (ct * 2 + h) * 3
                    hsrc, hk = cst[:, o:o + 3], "cst"
                else:
                    o = OS_CONV + (ct * 3 + 0) * 3
                    hsrc, hk = osm[:, o:o + 3], "osm"
                S.op("dve", lambda e: e.tensor_copy(out=xa[:, 0:3], in_=hsrc), r=[hk], w=[kxa], t=0.1)
                yield
                wc = lambda j: C("wca", 1, ct * 4 + j)
                S.op("dve", lambda e: e.tensor_scalar(out=Q["xc"][:, 0:N], in0=xa[:, 0:N], scalar1=wc(0),
                                                     scalar2=C("bca", 1, ct), op0=ALU.mult, op1=ALU.add),
                     r=[kxa, "cst"], w=["xc%d" % ci])
                yield
                for j in (1, 2, 3):
                    S.op("dve", lambda e, j=j: e.scalar_tensor_tensor(out=Q["xc"][:, 0:N], in0=xa[:, j:j + N],
                                                                     scalar=wc(j), in1=Q["xc"][:, 0:N], op0=ALU.mult,
                                                                     op1=ALU.add),
                         r=[kxa, "xc%d" % ci, "cst"], w=["xc%d" % ci])
                    yield
                o2 = OS_CONV + (ct * 3 + seq) * 3
                S.op("dve", lambda e: e.tensor_copy(out=osm[:, o2:o2 + 3], in_=xa[:, N:N + 3]), r=[kxa], w=["osm"], t=0.1)
                yield
                S.op("act", lambda e: e.activation(out=Q["xcb"][:, 0:N], in_=Q["xc"][:, 0:N], func=AF.Copy),
                     r=["xc%d" % ci], w=["xcb%d" % ci])
                yield
                S.op("pe", lambda e: e.matmul(bR[:, 0:N], lhsT=wr_bd[:, ct, :], rhs=Q["xcb"][:, 0:N], start=True,
                                              stop=True), r=["wr_bd", "xcb%d" % ci], w=["B%d" % zb[0]])
                S.op("pe", lambda e: e.matmul(bI[:, 0:N], lhsT=wi_bd[:, ct, :], rhs=Q["xcb"][:, 0:N], start=True,
                                              stop=True), r=["wi_bd", "xcb%d" % ci], w=["B%d" % zb[1]])
                yield
                S.op("act", lambda e: e.activation(out=Q["rr"][:, 0:N], in_=bR[:, 0:N], func=AF.Tanh, scale=0.5,
                                                   bias=clam[:, 8 + ct:9 + ct]), r=["B%d" % zb[0], "clam"], w=["rr%d" % ci], aset="t")
                yield
                S.op("act", lambda e: e.activation(out=Q["ii"][:, 0:N], in_=bI[:, 0:N], func=AF.Tanh, scale=0.5,
                                                   bias=clam[:, 12 + ct:13 + ct]), r=["B%d" % zb[1], "clam"], w=["ii%d" % ci], aset="t")
                yield

                yield from ch2b(ct, ci)

            def ch2b(ct, ci):
                Q, boff = chain_bufs(ct, ci)
                S.op("act", lambda e: e.activation(out=Q["aa"][:, 0:N], in_=Q["rr"][:, 0:N], func=AF.Exp,
                                                   scale=clam[:, ct:ct + 1], bias=clam[:, ct:ct + 1]),
                     r=["rr%d" % ci, "clam"], w=["aa%d" % ci], aset="e")
                yield
                S.op("act", lambda e: e.activation(out=Q["rr"][:, 0:N], in_=Q["rr"][:, 0:N], func=AF.Exp,
                                                   scale=clam[:, 4 + ct:5 + ct], bias=clam[:, 4 + ct:5 + ct]),
                     r=["rr%d" % ci, "clam"], w=["rr%d" % ci], aset="e")
                yield
                S.op("dve", lambda e: e.scalar_tensor_tensor(out=Q["ii"][:, 0:N], in0=Q["ii"][:, 0:N], scalar=1.0,
                                                            in1=Q["xc"][:, 0:N], op0=ALU.add, op1=ALU.mult),
                     r=["ii%d" % ci, "xc%d" % ci], w=["ii%d" % ci])
                yield
                S.op("act", lambda e: e.activation(out=Q["rr"][:, 0:N], in_=Q["rr"][:, 0:N], func=AF.Sqrt, scale=-1.0,
                                                   bias=1.0), r=["rr%d" % ci], w=["rr%d" % ci], aset="s")
                yield
                if (not isS) and h == 0 and st == 0:
                    S.op("dve", lambda e: e.memset(Q["rr"][:, 0:1], 1.0), w=["rr%d" % ci], t=0.08)
                    yield
                S.op("dve", lambda e: e.scalar_tensor_tensor(out=Q["rr"][:, 0:N], in0=Q["rr"][:, 0:N], scalar=0.5,
                                                            in1=Q["ii"][:, 0:N], op0=ALU.mult, op1=ALU.mult),
                     r=["rr%d" % ci, "ii%d" % ci], w=["rr%d" % ci])
                yield
                if isS:
                    o = _off["h0"] + ct * 2 + h
                    init, ik = cst[:, o:o + 1], "cst"
                else:
                    init, ik = osm[:, OS_H + ct * 3:OS_H + ct * 3 + 1], "osm"
                S.op("dve", lambda e: e.tensor_tensor_scan(out=Q["hbuf"][:, 0:N], data0=Q["aa"][:, 0:N],
                                                          data1=Q["rr"][:, 0:N], initial=init, op0=ALU.mult,
                                                          op1=ALU.add),
                     r=["aa%d" % ci, "rr%d" % ci, ik], w=["hbuf%d" % ci])
                yield
                o3 = OS_H + ct * 3 + seq
                S.op("dve", lambda e: e.tensor_copy(out=osm[:, o3:o3 + 1], in_=Q["hbuf"][:, N - 1:N]),
                     r=["hbuf%d" % ci], w=["osm"], t=0.15)
                yield
                S.op("dve", lambda e: e.scalar_tensor_tensor(out=yT[:, ct, 0:N], in0=Q["hbuf"][:, 0:N],
                                                            scalar=C("goa", 1, ct), in1=gg4[:, ct, 0:N],
                                                            op0=ALU.mult, op1=ALU.mult),
                     r=["hbuf%d" % ci, "gg%d" % ct, "cst"], w=["yTa%d" % ct])
                yield
                S.op("dve", lambda e: e.tensor_tensor(out=Q["sqa"][:, 0:N], in0=yT[:, ct, 0:N], in1=yT[:, ct, 0:N],
                                                     op=ALU.mult), r=["yTa%d" % ct], w=["sqa%d" % ci], t=0.4)
                yield
                for s in range(nsub):
                    S.op("pe", lambda e, s=s: e.matmul(B[0][:rows, s * 8 + ct:s * 8 + ct + 1],
                                                      lhsT=Q["sqa"][:, s * 128:s * 128 + rows],
                                                      rhs=invg2[:, ct:ct + 1], start=True, stop=True),
                         r=["sqa%d" % ci, "invg2"], w=["B0"], inc=(s == nsub - 1))
                yield

            dag = []
            for ct in range(4):
                dag.append(((lambda ct=ct: chz(ct)), [2 * ct - 2] if ct > 0 else []))
                if isS:
                    dag.append(((lambda ct=ct: ch2a(ct, ct)), [2 * ct]))
                else:
                    dag.append(((lambda ct=ct: ch2a(ct, ct % 2)), [2 * ct] + ([2 * ct - 3] if ct >= 2 else [])))
            run_dag(dag)
            S.fence(TP_KEYS, S2_KEYS)
            if h == 0 and st == 0:
                colsum_setup()

            def ch3(s, ci):
                c0 = s * 128
                T = TP[ci]
                vb, sbk = (2, 3) if ci == 0 else (4, 5)
                kv, ks = "B%d" % vb, "B%d" % sbk
                for kt in range(8):
                    S.op("pe", lambda e, kt=kt: e.matmul(B[vb][:rows, :], lhsT=hnT[:, kt, c0:c0 + rows],
                                                        rhs=win_sb[:, kt, 1536:2048], start=(kt == 0), stop=(kt == 7)),
                         r=["hnT%d" % s, "win%d" % kt], w=[kv], inc=(kt == 7))
                yield
                st6, kst6 = SM(ci, "st6")
                mv, kmv = SM(ci, "mv")
                rv, krv = SM(ci, "rv")
                nmr, knmr = SM(ci, "nmr")
                S.op("dve", lambda e: e.bn_stats(out=st6[:rows], in_=B[vb][:rows, :]), r=[kv], w=[kst6])
                yield
                S.op("dve", lambda e: e.bn_aggr(out=mv[:rows], in_=st6[:rows]), r=[kst6], w=[kmv], t=0.18)
                yield
                yield from rstd(rv, mv[:, 1:2], 1.0, rows, [kmv], krv)
                S.op("dve", lambda e: e.tensor_scalar(out=nmr[:rows], in0=mv[:rows, 0:1], scalar1=rv[:rows],
                                                     scalar2=-1.0, op0=ALU.mult, op1=ALU.mult),
                     r=[kmv, krv], w=[knmr], t=0.2)
                yield
                if isS:
                    S.op("act", lambda e: e.activation(out=T["vn"][:rows, :], in_=B[vb][:rows, :], func=AF.Identity,
                                                       scale=rv[:rows], bias=nmr[:rows]),
                         r=[kv, krv, knmr], w=["vn%d" % ci])
                    yield
                    S.op("dve", lambda e: e.tensor_tensor(out=T["vn"][:rows, :], in0=T["vn"][:rows, :],
                                                         in1=bc[:rows, BC_GV:BC_GV + 512], op=ALU.mult),
                         r=["vn%d" % ci, "bc"], w=["vn%d" % ci])
                    yield
                    S.op("dve", lambda e: e.tensor_tensor(out=T["vn"][:rows, :], in0=T["vn"][:rows, :],
                                                         in1=bc[:rows, BC_BV:BC_BV + 512], op=ALU.add),
                         r=["vn%d" % ci, "bc"], w=["vn%d" % ci])
                    yield
                    S.op("act", lambda e: e.activation(out=T["vnb"][:rows, :], in_=T["vn"][:rows, :], func=AF.Copy),
                         r=["vn%d" % ci], w=["vnb%d" % ci])
                    S.dma("sp", "vns", vns_d[h, :, :], T["vn"][:rows, :], r=["vn%d" % ci])
                    yield
                    for hd in range(4):
                        S.op("pe", lambda e, hd=hd: e.matmul(B[sbk][:, hd * 128:hd * 128 + rows],
                                                            lhsT=T["vnb"][:rows, hd * 128:(hd + 1) * 128],
                                                            rhs=wsS[:rows, hd, 0:rows], start=True, stop=False),
                             r=["vnb%d" % ci, "wsS"], w=[ks], inc=False)
                        S.op("pe", lambda e, hd=hd: e.matmul(B[sbk][:, hd * 128:hd * 128 + rows], lhsT=ones_r[0:1, :],
                                                            rhs=bsr[0:1, hd, 0:rows], start=False, stop=True),
                             r=["ones_r", "bsr"], w=[ks], inc=(hd == 3))
                    yield
                    gsc = lambda hd: C("gob", 1, hd)
                else:
                    S.op("act", lambda e: e.activation(out=T["vnb"][:rows, :], in_=B[vb][:rows, :], func=AF.Identity,
                                                       scale=rv[:rows], bias=nmr[:rows]),
                         r=[kv, krv, knmr], w=["vnb%d" % ci])
                    yield
                    for hd in range(4):
                        S.op("pe", lambda e, hd=hd: e.matmul(B[sbk][:, hd * 128:(hd + 1) * 128],
                                                            lhsT=T["vnb"][:, hd * 128:(hd + 1) * 128],
                                                            rhs=wsP[:, hd, :], start=True, stop=False),
                             r=["vnb%d" % ci, "wsP"], w=[ks], inc=False)
                        S.op("pe", lambda e, hd=hd: e.matmul(B[sbk][:, hd * 128:(hd + 1) * 128],
                                                            lhsT=rowb[0:1, 512 + hd * 128:512 + (hd + 1) * 128],
                                                            rhs=rowb[0:1, 1024 + hd * 128:1024 + (hd + 1) * 128],
                                                            start=False, stop=False),
                             r=["rowb_b", "rowb_cs"], w=[ks], inc=False)
                        S.op("pe", lambda e, hd=hd: e.matmul(B[sbk][:, hd * 128:(hd + 1) * 128],
                                                            lhsT=rowb[0:1, hd * 128:(hd + 1) * 128],
                                                            rhs=rowb[0:1, 1536 + hd * 128:1536 + (hd + 1) * 128],
                                                            start=False, stop=True),
                             r=["rowb_g", "rowb_bs"], w=[ks], inc=(hd == 3))
                    yield
                    gsc = lambda hd: gvgob[:, hd:hd + 1]
                for hd in range(4):
                    S.op("dve", lambda e, hd=hd: e.scalar_tensor_tensor(
                        out=yT[:, 4 + hd, c0:c0 + rows], in0=B[sbk][:, hd * 128:hd * 128 + rows],
                        scalar=gsc(hd), in1=u_sb[:, hd, c0:c0 + rows], op0=ALU.mult, op1=ALU.mult),
                         r=[ks, "u_sb", "cst", "gvgob"], w=["yTb%d" % s])
                    yield
                S.op("dve", lambda e: e.tensor_tensor(out=T["sqb"][:, :, 0:rows], in0=yT[:, 4:8, c0:c0 + rows],
                                                     in1=yT[:, 4:8, c0:c0 + rows], op=ALU.mult),
                     r=["yTb%d" % s], w=["sqb%d" % ci], t=0.4)
                yield
                for hd in range(4):
                    S.op("pe", lambda e, hd=hd: e.matmul(B[0][:rows, s * 8 + 4 + hd:s * 8 + 5 + hd],
                                                        lhsT=T["sqb"][:, hd, 0:rows], rhs=invg2[:, 4 + hd:5 + hd],
                                                        start=True, stop=True),
                         r=["sqb%d" % ci, "invg2"], w=["B0"], inc=(hd == 3))
                yield

            run_window([(lambda s=s: ch3(s, s % 2)) for s in range(nsub)], lag=7)

            S.op("dve", lambda e: e.tensor_reduce(out=SS2[:rows, 0:2 * nsub],
                                                 in_=B[0][:rows, 0:8 * nsub].rearrange("p (a b) -> p a b", b=4),
                                                 axis=AX.X, op=ALU.add), r=["B0"], w=["ss2"])
            for _ in rstd(RAB[:, 0:2 * nsub], SS2[:, 0:2 * nsub], 1.0 / DA, rows, ["ss2"], "rab"):
                pass
            YK = ["yTa%d" % c for c in range(4)] + ["yTb%d" % s for s in range(nsub)]

            def ch4(s, ci):
                tt = tts[s]
                c0 = s * 128
                xk = "x1_%d" % tt
                T = TP[ci]
                pa, pb = (2, 3) if ci == 0 else (4, 5)
                for nh in range(2):
                    for grp, bk in ((0, pa), (1, pb)):
                        for k in range(4):
                            kt = grp * 4 + k
                            S.op("pe", lambda e, kt=kt, bk=bk, nh=nh, k=k: e.matmul(
                                B[bk][:rows, :], lhsT=yT[:, kt, c0:c0 + rows], rhs=wo_sb[:, kt, nh * 512:(nh + 1) * 512],
                                start=(k == 0), stop=(k == 3)), r=YK + ["wo%d" % (kt // 2)], w=["B%d" % bk],
                                 inc=(k == 3))
                    yield
                    S.op("act", lambda e, nh=nh: e.activation(out=T["tmpA"][:rows, :], in_=B[pa][:rows, :], func=AF.Copy,
                                                             scale=RAB[:rows, 2 * s:2 * s + 1]),
                         r=["B%d" % pa, "rab"], w=["tmpA%d" % ci])
                    yield
                    S.op("dve", lambda e, nh=nh: e.scalar_tensor_tensor(
                        out=T["mix"][:rows, nh * 512:(nh + 1) * 512], in0=B[pb][:rows, :],
                        scalar=RAB[:rows, 2 * s + 1:2 * s + 2], in1=T["tmpA"][:rows, :],
                        op0=ALU.mult, op1=ALU.add), r=["B%d" % pb, "tmpA%d" % ci, "rab"], w=["mix%d" % ci])
                    yield
                ssm, kssm = SM(ci, "ssm")
                rm, krm = SM(ci, "rm")
                S.op("act", lambda e: e.activation(out=T["junk"][:rows, :], in_=T["mix"][:rows, :], func=AF.Square,
                                                   accum_out=ssm[:rows]), r=["mix%d" % ci], w=["junk%d" % ci, kssm])
                yield
                yield from rstd(rm, ssm, 1.0 / D, rows, [kssm], krm)
                S.op("dve", lambda e: e.scalar_tensor_tensor(out=T["mix"][:rows, :], in0=T["mix"][:rows, :],
                                                            scalar=rm[:rows], in1=bc[:rows, BC_GP1:BC_GP1 + D],
                                                            op0=ALU.mult, op1=ALU.mult),
                     r=["mix%d" % ci, krm, "bc"], w=["mix%d" % ci])
                yield
                S.op("dve", lambda e: e.tensor_tensor(out=x1[:rows, tt, :], in0=T["mix"][:rows, :],
                                                     in1=x1[:rows, tt, :], op=ALU.add),
                     r=["mix%d" % ci, xk], w=[xk])
                yield
                col0 = (2 + NPH) if isS else (2 + tt * 128)
                yield from ch_norm_transpose(tt, rows, ci, hn2T, col0, "gpre2", "hn2T_%d" % tt)

            dag = [((lambda s=s: ch4(s, s % 2)), [s - 2] if s >= 2 else []) for s in range(nsub)]
            if st < 2:
                nS = (st + 1 == 2)
                n_rows = 64 if nS else 128
                n_tts = [8] if nS else [(st + 1) * 4 + s for s in range(4)]
                base = len(dag)
                for s2, tt2 in enumerate(n_tts):
                    dag.append(((lambda s2=s2, tt2=tt2: ch_norm_transpose(tt2, n_rows, 2, hnT, s2 * 128, "gpre1",
                                                                         "hnT%d" % s2)),
                                [base + s2 - 1] if s2 > 0 else []))
            run_dag(dag)

    def phase_B(h):
        n_last = 341
        tiles = [(0, 342, False), (342, 341, False), (683, n_last, True)]

        def wup_dma(j):
            slot = j % 3
            S.dma("pool", "wup%d" % slot, wup[slot][:, :, :].rearrange("p a b -> p (a b)"), wup_d[j, :, :],
                  w=["wup%d" % slot])

        wup_dma(0)
        wup_dma(1)
        tcount = 0
        for j in range(24):
            slot = j % 3
            if j + 2 < 24:
                wup_dma(j + 2)
            if j % 2 == 1:
                jd = j // 2
                S.dma("pool", "wdn%d" % jd, wdn_sb[:, 2 * jd:2 * jd + 2, :], wdn_d[:, 2 * jd:2 * jd + 2, :],
                      w=["wdn%d" % jd])
            cg, cl = j, 24 + j
            for (t0, n, withS) in tiles:
                par = tcount % NSET
                tcount += 1
                bg, bl = B[2 * par], B[2 * par + 1]
                kg, kl = "B%d" % (2 * par), "B%d" % (2 * par + 1)
                ag, al = accg[par], accl[par]
                kag, kal = "accg%d" % par, "accl%d" % par
                nq = n + 2 + NSH if withS else n
                if withS:
                    o = _off["ffnst"]
                    for (bk, kk, cc) in ((bg, kg, cg), (bl, kl, cl)):
                        src = cst[:, o + (cc * 2 + h) * 2:o + (cc * 2 + h) * 2 + 2]
                        S.op("pe", lambda e, bk=bk, src=src: e.matmul(bk[:, n + 2:n + 4], lhsT=ident[:, :], rhs=src,
                                                                     start=True, stop=True),
                             r=["cst", "ident"], w=[kk], inc=False)
                for (bk, kk, half) in ((bg, kg, 0), (bl, kl, 1)):
                    for kt in range(8):
                        S.op("pe", lambda e, bk=bk, kt=kt, half=half: e.matmul(
                            bk[:, 0:n + 2], lhsT=wup[slot][:, kt, half * 128:(half + 1) * 128],
                            rhs=hn2T[:, kt, t0:t0 + n + 2], start=(kt == 0), stop=(kt == 7)),
                             r=["wup%d" % slot] + HN2_KEYS, w=[kk], inc=(kt == 7))
                    if withS:
                        for kt in range(8):
                            S.op("pe", lambda e, bk=bk, kt=kt, half=half: e.matmul(
                                bk[:, n + 4:n + 4 + NSH], lhsT=wup[slot][:, kt, half * 128:(half + 1) * 128],
                                rhs=hn2T[:, kt, 2 + NPH:2 + NPH + NSH], start=(kt == 0), stop=(kt == 7)),
                                 r=["wup%d" % slot] + HN2_KEYS, w=[kk], inc=(kt == 7))
                branches = ((bg, kg, ag, kag, cg), (bl, kl, al, kal, cl))
                w3 = lambda tap, cc: C("wcf", 1, cc * 3 + tap)
                for (bk, kk, acc, ka, cc) in branches:
                    S.op("act", lambda e, bk=bk, acc=acc, cc=cc: e.activation(out=acc[:, 0:nq], in_=bk[:, 0:nq],
                                                                            func=AF.Identity, scale=w3(0, cc),
                                                                            bias=C("bcf", 1, cc)),
                         r=[kk, "cst"], w=[ka])
                for tap in (1, 2):
                    for (bk, kk, acc, ka, cc) in branches:
                        S.op("dve", lambda e, bk=bk, acc=acc, tap=tap, cc=cc: e.scalar_tensor_tensor(
                            out=acc[:, 0:nq], in0=bk[:, tap:tap + nq], scalar=w3(tap, cc), in1=acc[:, 0:nq],
                            op0=ALU.mult, op1=ALU.add), r=[kk, ka, "cst"], w=[ka])
                for (bk, kk, acc, ka, cc) in branches:
                    if withS:
                        oo = OS_FFN + (cc * 3 + 1 + h) * 2
                        S.op("dve", lambda e, bk=bk, oo=oo: e.tensor_copy(out=osm[:, oo:oo + 2], in_=bk[:, n + 66:n + 68]),
                             r=[kk], w=["osmS%d" % cc])
                        if h == 1:
                            oo = OS_FFN + (cc * 3 + 0) * 2
                            S.op("dve", lambda e, bk=bk, oo=oo: e.tensor_copy(out=osm[:, oo:oo + 2], in_=bk[:, n:n + 2]),
                                 r=[kk], w=["osmP%d" % cc])
                S.op("act", lambda e, ag=ag: e.activation(out=ag[:, 0:nq], in_=ag[:, 0:nq], func=AF.Gelu_apprx_tanh),
                     r=[kag], w=[kag])
                S.op("pool", lambda e, ag=ag, al=al: e.tensor_tensor(out=gT[:, j, t0:t0 + nq], in0=ag[:, 0:nq],
                                                                    in1=al[:, 0:nq], op=ALU.mult),
                     r=[kag, kal], w=["gT%d" % j])

    def phase_C(h):
        for tt in range(NTT):
            rows = 64 if tt == 8 else 128
            c0 = (NPH + 2) if tt == 8 else tt * 128
            par = tt % 2
            xk = "x1_%d" % tt
            ssf, kssf = SM(par, "ssf")
            rf, krf = SM(par, "rf")
            for nh in range(2):
                bk = 4 + 2 * par + nh
                for c in range(24):
                    S.op("pe", lambda e, bk=bk, c=c, nh=nh: e.matmul(B[bk][:rows, :], lhsT=gT[:, c, c0:c0 + rows],
                                                                    rhs=wdn_sb[:, c, nh * 512:(nh + 1) * 512],
                                                                    start=(c == 0), stop=(c == 23)),
                         r=["gT%d" % c, "wdn%d" % (c // 2)], w=["B%d" % bk], inc=(c == 23))
            for nh in range(2):
                bk = 4 + 2 * par + nh
                S.op("act", lambda e, bk=bk, nh=nh: e.activation(out=junk2[:rows, :], in_=B[bk][:rows, :],
                                                                func=AF.Square, accum_out=ssf[:rows, nh:nh + 1]),
                     r=["B%d" % bk], w=["hn2T_0", "hn2T_1", "hn2T_2", "hn2T_3", kssf])
            S.op("dve", lambda e: e.tensor_tensor(out=rf[:rows], in0=ssf[:rows, 0:1], in1=ssf[:rows, 1:2], op=ALU.add),
                 r=[kssf], w=[krf])
            for _ in rstd(rf, rf, 1.0 / D, rows, [krf], krf):
                pass
            t2h = (accg[par], accl[par])
            t2k = ("accg%d" % par, "accl%d" % par)
            for nh in range(2):
                bk = 4 + 2 * par + nh
                S.op("dve", lambda e, bk=bk, nh=nh: e.scalar_tensor_tensor(
                    out=t2h[nh][:rows, :], in0=B[bk][:rows, :], scalar=rf[:rows],
                    in1=bc[:rows, BC_GP2 + nh * 512:BC_GP2 + (nh + 1) * 512], op0=ALU.mult, op1=ALU.mult),
                     r=["B%d" % bk, krf, "bc"], w=[t2k[nh]])
                S.op("pool", lambda e, nh=nh: e.tensor_tensor(out=x1[:rows, tt, nh * 512:(nh + 1) * 512],
                                                             in0=t2h[nh][:rows, :],
                                                             in1=x1[:rows, tt, nh * 512:(nh + 1) * 512], op=ALU.add),
                     r=[t2k[nh], xk], w=[xk])
            dst = y_d[16, h * 64:(h + 1) * 64, :] if tt == 8 else y_d[h * 8 + tt, :, :]
            S.dma("sp", "yout%d" % tt, dst, x1[:rows, tt, :], r=[xk])

    for h in range(2):
        if h == 1:
            S.op("dve", lambda e: e.tensor_copy(out=hn2T[:, :, 0:2], in_=hn2T[:, :, NPH:NPH + 2]),
                 r=HN2_KEYS, w=["hn2T_h"])
        S.fence(WA_KEYS, WD_KEYS)
        for kt in range(8):
            S.dma("pool", "win%d" % kt, win_sb[:, kt, :], win_d[:, kt, :],
                  r=(["x1_3"] if (h == 0 and kt == 0) else []), w=["win%d" % kt])
        if h == 0:
            small_weight_dmas()
        for k2 in range(4):
            S.dma("pool", "wo%d" % k2, wo_sb[:, 2 * k2:2 * k2 + 2, :], wo_d[:, 2 * k2:2 * k2 + 2, :], w=["wo%d" % k2])
        S.fence(ARENA_KEYS_A, ARENA_KEYS_B + GT_KEYS)
        phase_A(h)
        if _STOP == "A%d" % h:
            break
        S.fence(["wup0", "wup1", "wup2"], ["hnT%d" % k for k in range(4)] + ["u_sb"])
        S.fence([k for k in ARENA_KEYS_B if not k.startswith("wup")] + GT_KEYS, ARENA_KEYS_A)
        S.fence(WD_KEYS, WA_KEYS)
        phase_B(h)
        if _STOP == "B%d" % h:
            break
        phase_C(h)
    S.dma("sp", "osm", osm_d[:, :], osm[:], r=["osm"] + ["osmS%d" % c for c in range(48)] + ["osmP%d" % c for c in range(48)])
    S.finish("sp")


def _fm(v, ntile):
    return np.ascontiguousarray(np.asarray(v, np.float32).reshape(ntile, 128).T)


_NC_CACHE = {}


def kernel(x_prompt, x_sample, state_lru_h, state_lru_conv, state_ffn_conv,
           g_pre1, w_in, w_conv_a, b_conv_a, w_r, b_r, w_i, b_i, lam, g_out_a,
           g_v, b_v, w_s, b_s, g_out_b, w_o, g_post1,
           g_pre2, w_up, w_conv_f, b_conv_f, w_down, g_post2):
    f = lambda a: np.asarray(a, np.float32)
    x_prompt, x_sample = f(x_prompt), f(x_sample)
    L = 0
    win_r = np.ascontiguousarray(f(w_in)[L].reshape(8, 128, 2048).transpose(1, 0, 2))
    wo_r = np.ascontiguousarray(f(w_o)[L].reshape(8, 128, 1024).transpose(1, 0, 2))
    wu = f(w_up)[L].reshape(8, 128, 2, 24, 128)
    wup_r = np.ascontiguousarray(wu.transpose(3, 1, 0, 2, 4)).reshape(24, 128, 2048)
    wdn_r = np.ascontiguousarray(f(w_down)[L].reshape(24, 128, 1024).transpose(1, 0, 2))
    wsT = np.ascontiguousarray(f(w_s)[L].transpose(0, 2, 1))
    bs_r = np.ascontiguousarray(f(b_s)[L].reshape(1, 4, 128))
    ident = np.eye(128, dtype=np.float32)
    bcv = np.concatenate([f(g_v)[L], f(b_v)[L], f(g_post1)[L], f(g_post2)[L]])
    bc = np.ascontiguousarray(np.broadcast_to(bcv[None, :], (128, KBC)))

    cst_common = np.zeros((128, KCST), np.float32)

    def put(name, arr):
        arr = np.asarray(arr, np.float32).reshape(128, -1)
        cst_common[:, _off[name]:_off[name] + arr.shape[1]] = arr

    put("gpre1", _fm(f(g_pre1)[L], 8))
    put("gpre2", _fm(f(g_pre2)[L], 8))
    put("wca", f(w_conv_a)[L].reshape(4, 4, 128).transpose(2, 1, 0))
    put("bca", _fm(f(b_conv_a)[L], 4))
    put("br", _fm(f(b_r)[L], 4))
    put("bi", _fm(f(b_i)[L], 4))
    put("lam", _fm(f(lam)[L], 4))
    put("goa", _fm(f(g_out_a)[L], 4))
    put("gob", _fm(f(g_out_b)[L], 4))
    put("gv", _fm(f(g_v)[L], 4))
    put("wcf", f(w_conv_f)[L].reshape(3, 48, 128).transpose(2, 1, 0))
    put("bcf", _fm(f(b_conv_f)[L], 48))

    in_maps = []
    for c in range(NCORES):
        cst = cst_common.copy()

        def putc(name, arr):
            arr = np.asarray(arr, np.float32).reshape(128, -1)
            cst[:, _off[name]:_off[name] + arr.shape[1]] = arr

        sh = f(state_lru_h)[L, 2 * c:2 * c + 2]
        putc("h0", sh.reshape(2, 4, 128).transpose(2, 1, 0))
        sc = f(state_lru_conv)[L, 2 * c:2 * c + 2]
        putc("lconv", sc.reshape(2, 3, 4, 128).transpose(3, 2, 0, 1))
        sf = f(state_ffn_conv)[L, 2 * c:2 * c + 2]
        putc("ffnst", sf.reshape(2, 2, 48, 128).transpose(3, 2, 0, 1))
        xt = np.concatenate([x_prompt[c].reshape(16, 128, D),
                             np.concatenate([x_sample[2 * c], x_sample[2 * c + 1]], 0)[None]], 0)
        in_maps.append({
            "x": np.ascontiguousarray(xt), "cst": cst, "bc": bc, "ident": ident,
            "w_in": win_r, "w_o": wo_r, "w_up": wup_r, "w_down": wdn_r,
            "w_r": np.ascontiguousarray(f(w_r)[L]), "w_i": np.ascontiguousarray(f(w_i)[L]),
            "w_sT": wsT, "b_s": bs_r,
        })

    if "nc" not in _NC_CACHE:
        _NC_CACHE["nc"] = build_program()
    nc = _NC_CACHE["nc"]
    res = run_bass_kernel_spmd(nc, in_maps, core_ids=list(range(NCORES)))
    R = res.results

    y_prompt = np.zeros((8, 2048, D), np.float32)
    y_sample = np.zeros((16, 64, D), np.float32)
    hp = np.zeros((1, 8, DA), np.float32)
    cp = np.zeros((1, 8, 3, DA), np.float32)
    fp = np.zeros((1, 8, 2, 2 * DFF), np.float32)
    hs = np.zeros((1, 16, DA), np.float32)
    cs = np.zeros((1, 16, 3, DA), np.float32)
    fs = np.zeros((1, 16, 2, 2 * DFF), np.float32)
    vs = np.zeros((1, 16, 64, 512), np.float32)
    for c in range(NCORES):
        y = np.asarray(R[c]["y"])
        y_prompt[c] = y[:16].reshape(2048, D)
        y_sample[2 * c] = y[16, :64]
        y_sample[2 * c + 1] = y[16, 64:]
        vs[0, 2 * c:2 * c + 2] = np.asarray(R[c]["vns"])
        o = np.asarray(R[c]["osm"])
        oh = o[:, OS_H:OS_H + 12].reshape(128, 4, 3)
        oc = o[:, OS_CONV:OS_CONV + 36].reshape(128, 4, 3, 3)
        of = o[:, OS_FFN:OS_FFN + 288].reshape(128, 48, 3, 2)
        hseq = oh.transpose(2, 1, 0).reshape(3, DA)
        cseq = oc.transpose(2, 3, 1, 0).reshape(3, 3, DA)
        fseq = of.transpose(2, 3, 1, 0).reshape(3, 2, 2 * DFF)
        hp[0, c], cp[0, c], fp[0, c] = hseq[0], cseq[0], fseq[0]
        hs[0, 2 * c:2 * c + 2] = hseq[1:]
        cs[0, 2 * c:2 * c + 2] = cseq[1:]
        fs[0, 2 * c:2 * c + 2] = fseq[1:]
    return (y_prompt, y_sample, hp, cp, fp, hs, cs, fs, vs)
```

```python
import numpy as np
from contextlib import ExitStack
import concourse.bass as bass
import concourse.mybir as mybir
from concourse.bass_utils import run_bass_kernel_spmd

F32 = mybir.dt.float32
BF16 = mybir.dt.bfloat16
AF = mybir.ActivationFunctionType
ALU = mybir.AluOpType
AX = mybir.AxisListType

NCORES = 8
import os as _os
_STOP = _os.environ.get("KSTOP", "")
D = 1024
DA = 512
DFF = 3072
EPS = 1e-6
NPH = 1024
NSH = 64
NTT = 9
HCOLS = 2 + NPH + NSH
GCOLS = NPH + 2 + NSH

_off = {}
_k = 0
for _n, _w in [("gpre1", 8), ("gpre2", 8), ("wca", 16), ("bca", 4), ("br", 4), ("bi", 4), ("lam", 4),
               ("goa", 4), ("gob", 4), ("gv", 4), ("wcf", 144), ("bcf", 48), ("h0", 8), ("lconv", 24), ("ffnst", 192)]:
    _off[_n] = _k
    _k += _w
KCST = _k
BC_GV, BC_BV, BC_GP1, BC_GP2, KBC = 0, 512, 1024, 2048, 3072
OS_H, OS_CONV, OS_FFN, KOSM = 0, 12, 48, 48 + 288


class Sched:
    def __init__(self, nc, es):
        self.nc = nc
        self.es = es
        self.eng = {"pe": nc.tensor, "dve": nc.vector, "act": nc.scalar, "pool": nc.gpsimd, "sp": nc.sync}
        self.sem = {}
        self.cnt = {}
        for e in ("pe", "dve", "act", "pool"):
            self.sem[e] = es.enter_context(nc.semaphore("c_" + e))
            self.cnt[e] = 0
        self.dsem = {}
        self.dcnt = {}
        self.waited = {}
        self.buf = {}
        self.nsem = 0
        self.defer = None
        self.eng_free = {"pe": 0.0, "dve": 0.0, "act": 0.0, "pool": 0.0, "sp": 0.0}
        self.act_set = None

    def _dma_sem(self, name):
        if name not in self.dsem:
            self.dsem[name] = self.es.enter_context(self.nc.semaphore("d_" + name))
            self.dcnt[name] = 0
        return self.dsem[name]

    def _b(self, k):
        if k not in self.buf:
            self.buf[k] = {"w": None, "r": {}}
        return self.buf[k]

    def _deps(self, r, w):
        deps = []
        for k in r:
            b = self._b(k)
            if b["w"] is not None:
                deps.append(b["w"])
        for k in w:
            b = self._b(k)
            if b["w"] is not None:
                deps.append(b["w"])
            deps.extend(b["r"].values())
        return deps

    def _wait(self, e, deps):
        best = {}
        for (sid, sem, val, src) in deps:
            if src == "pe" and e == "pe":
                continue
            if self.waited.get((e, sid), 0) >= val:
                continue
            if best.get(sid, (None, 0))[1] < val:
                best[sid] = (sem, val)
        for sid, (sem, val) in best.items():
            self.eng[e].wait_ge(sem, val)
            self.waited[(e, sid)] = val

    def _record(self, tok, r, w):
        for k in r:
            b = self._b(k)
            old = b["r"].get(tok[0])
            if old is None or old[2] < tok[2]:
                b["r"][tok[0]] = tok
        for k in w:
            b = self._b(k)
            b["w"] = tok
            b["r"] = {}

    DEF_T = {"pe": 0.27, "dve": 0.6, "act": 0.8, "pool": 1.5, "sp": 0.1}

    ASETS = {"g": (11,), "t": (11, 0), "e": (0,), "s": (3,)}

    def op(self, e, fn, r=(), w=(), inc=True, t=None, aset=None):
        xb = [k for k in r if k[0] == "B" and k[1:].isdigit() and k not in w]
        if xb:
            w = tuple(w) + tuple(xb)
        if self.defer is not None:
            self.defer.append(("op", e, fn, tuple(r), tuple(w), inc, self.DEF_T[e] if t is None else t, aset))
            return None
        self._wait(e, self._deps(r, w))
        inst = fn(self.eng[e])
        if inc:
            self.cnt[e] += 1
            inst.then_inc(self.sem[e], 1)
            tok = ("c_" + e, self.sem[e], self.cnt[e], e)
        else:
            tok = ("c_" + e, self.sem[e], self.cnt[e] + 1, e)
        self._record(tok, r, w)
        return inst

    def dma(self, q, sname, out, in_, r=(), w=(), **kw):
        if self.defer is not None:
            self.defer.append(("dma", q, (sname, out, in_, kw), tuple(r), tuple(w), True, 0.1, None))
            return None
        sem = self._dma_sem(sname)
        self._wait(q, self._deps(r, w))
        inst = self.eng[q].dma_start(out=out, in_=in_, **kw)
        self.dcnt[sname] += 16
        inst.then_inc(sem, 16)
        tok = ("d_" + sname, sem, self.dcnt[sname], "dma")
        self._record(tok, r, w)
        return inst

    def fence(self, new_keys, old_keys):
        toks = {}
        for k in old_keys:
            b = self._b(k)
            cands = list(b["r"].values())
            if b["w"] is not None:
                cands.append(b["w"])
            for t in cands:
                if t[0] not in toks or toks[t[0]][2] < t[2]:
                    toks[t[0]] = t
        for k in new_keys:
            nb = self._b(k)
            for sid, t in toks.items():
                if sid not in nb["r"] or nb["r"][sid][2] < t[2]:
                    nb["r"][sid] = t

    def group_written(self, sname, keys):
        tok = ("d_" + sname, self.dsem[sname], self.dcnt[sname], "dma")
        for k in keys:
            b = self._b(k)
            b["w"] = tok
            b["r"] = {}

    def finish(self, e="sp"):
        for name, sem in self.dsem.items():
            if self.dcnt[name] > 0:
                self.eng[e].wait_ge(sem, self.dcnt[name])


def build_program():
    nc = bass.Bass("TRN2", target_bir_lowering=False)
    es = ExitStack()
    with es:
        _build(nc, es)
    return nc


def _build(nc, es):
    S = Sched(nc, es)

    def dram(name, shape, kind):
        return nc.dram_tensor(name, list(shape), F32, kind=kind).ap()

    x_d = dram("x", [17, 128, D], "ExternalInput")
    cst_d = dram("cst", [128, KCST], "ExternalInput")
    bc_d = dram("bc", [128, KBC], "ExternalInput")
    ident_d = dram("ident", [128, 128], "ExternalInput")
    win_d = dram("w_in", [128, 8, 2048], "ExternalInput")
    wo_d = dram("w_o", [128, 8, 1024], "ExternalInput")
    wup_d = dram("w_up", [24, 128, 2048], "ExternalInput")
    wdn_d = dram("w_down", [128, 24, 1024], "ExternalInput")
    wr_d = dram("w_r", [8, 64, 64], "ExternalInput")
    wi_d = dram("w_i", [8, 64, 64], "ExternalInput")
    wst_d = dram("w_sT", [4, 128, 128], "ExternalInput")
    bs_d = dram("b_s", [1, 4, 128], "ExternalInput")
    y_d = dram("y", [17, 128, D], "ExternalOutput")
    vns_d = dram("vns", [2, 64, 512], "ExternalOutput")
    osm_d = dram("osm", [128, KOSM], "ExternalOutput")

    def sb(name, shape, dt=F32):
        return es.enter_context(nc.sbuf_tensor("sb_" + name, list(shape), dt))

    def ps(name):
        return es.enter_context(nc.psum_tensor(name, [128, 512], F32))

    cst = sb("cst", [128, KCST])
    bc = sb("bc", [128, KBC])
    ident = sb("ident", [128, 128])
    wr_bd = sb("wr_bd", [128, 4, 128], BF16)
    wi_bd = sb("wi_bd", [128, 4, 128], BF16)
    wsP = sb("wsP", [128, 4, 128], BF16)
    wsS = sb("wsS", [64, 4, 64], BF16)
    bsr = sb("bsr", [1, 4, 128], BF16)
    ones_r = sb("ones_r", [1, 128], BF16)
    ones_c = sb("ones_c", [128, 1], BF16)
    igf = sb("igf", [1, 512])
    rowb = sb("rowb", [1, 2048], BF16)
    gvgob = sb("gvgob", [128, 4])
    clam = sb("clam", [128, 16])
    invg2 = sb("invg2", [128, 8], BF16)
    smt = sb("smt", [128, 16])
    osm = sb("osm", [128, KOSM])
    x1 = sb("x1", [128, NTT, D])
    hn2T = sb("hn2T", [128, 8, HCOLS], BF16)
    r48 = sb("r48", [128, 24 * 1024], BF16)
    ARENA = 77 * 1024
    arena = sb("arena", [128, ARENA // 4])

    class Carve:
        def __init__(self, off=0):
            self.off = off

        def take(self, shape, dt):
            esz = 4 if dt == F32 else 2
            n = int(np.prod(shape[1:]))
            nbytes = (n * esz + 31) // 32 * 32
            assert self.off + nbytes <= ARENA, ("arena overflow", self.off, nbytes)
            base = arena[:, self.off // 4:(self.off + nbytes) // 4]
            self.off += nbytes
            v = base.bitcast(dt) if dt != F32 else base
            v = v[:, 0:n]
            if len(shape) == 3:
                v = v.rearrange("p (a b) -> p a b", a=shape[1])
            return v

    ca = Carve()
    hnT = ca.take([128, 8, 512], BF16)
    u_sb = ca.take([128, 4, 512], F32)
    yT = ca.take([128, 8, 512], BF16)
    split = ca.off
    gg4 = ca.take([128, 4, 512], F32)
    xa4 = ca.take([128, 4, 520], F32)
    S2 = []
    for ci in range(2):
        S2.append(dict(xc=ca.take([128, 512], F32), xcb=ca.take([128, 512], BF16), rr=ca.take([128, 512], F32),
                       ii=ca.take([128, 512], F32), aa=ca.take([128, 512], F32), hbuf=ca.take([128, 512], F32),
                       sqa=ca.take([128, 512], BF16)))
    ct_ = Carve(split)
    TP = []
    for ci in range(3):
        TP.append(dict(xn=ct_.take([128, D], F32), junk=ct_.take([128, D], BF16), vn=ct_.take([128, 512], F32),
                       vnb=ct_.take([128, 512], BF16), tmpA=ct_.take([128, 512], F32), mix=ct_.take([128, D], F32),
                       sqb=ct_.take([128, 4, 128], BF16)))
    S2_KEYS = ["gg%d" % c for c in range(4)] + ["xa%d" % c for c in range(4)] + \
              [n + str(ci) for ci in range(4) for n in ("xc", "xcb", "rr", "ii", "aa", "hbuf", "sqa")]
    TP_KEYS = [n + str(ci) for ci in range(3) for n in ("xn", "junk", "vn", "vnb", "tmpA", "mix", "sqb")]
    SH_KEYS = ["hnT%d" % s for s in range(4)] + ["u_sb"] + ["yTa%d" % c for c in range(4)] + ["yTb%d" % c for c in range(4)]
    cb = Carve()
    wup = [cb.take([128, 8, 256], BF16) for _ in range(3)]
    gT = cb.take([128, 24, GCOLS], BF16)
    NSET = 3
    accg = [cb.take([128, 512], F32) for _ in range(NSET)]
    accl = [cb.take([128, 512], F32) for _ in range(NSET)]
    junk2 = hn2T[:, 0, 2:514]
    ARENA_KEYS_A = S2_KEYS + TP_KEYS + SH_KEYS
    ARENA_KEYS_B = ["wup0", "wup1", "wup2"] + ["accg%d" % i for i in range(NSET)] + \
                   ["accl%d" % i for i in range(NSET)]
    GT_KEYS = ["gT%d" % k for k in range(24)]
    WA_KEYS = ["win%d" % k for k in range(8)] + ["wo%d" % k for k in range(4)]
    WD_KEYS = ["wdn%d" % k for k in range(12)]
    HN2_KEYS = ["hn2T_%d" % t for t in range(NTT)] + ["hn2T_h"]

    win_sb = r48[:, 0:8 * 2048].rearrange("p (a b) -> p a b", a=8)
    wo_sb = r48[:, 8 * 2048:8 * 2048 + 8 * 1024].rearrange("p (a b) -> p a b", a=8)
    wdn_sb = r48[:, :].rearrange("p (a b) -> p a b", a=24)

    B = [ps("bank%d" % i) for i in range(8)]

    def C(name, width=1, idx=0):
        o = _off[name] + idx
        return cst[:, o:o + width]

    def run_chains(gens):
        gens = list(gens)
        while gens:
            for g in list(gens):
                try:
                    next(g)
                except StopIteration:
                    gens.remove(g)

    def run_window(makers, lag=0, window=2):
        run_dag([(m, [k - window] if k >= window else []) for k, m in enumerate(makers)])

    def run_dag(chains):
        LAT = 0.2
        TOL = 0.0
        n = len(chains)
        state = [None] * n
        ndone = 0

        def pull(ch):
            ch["pending"] = []
            S.defer = ch["pending"]
            try:
                next(ch["gen"])
            except StopIteration:
                ch["done"] = True
            S.defer = None

        while ndone < n:
            for k in range(n):
                if state[k] is None and all(state[a_] == "done" for a_ in chains[k][1]):
                    ch = {"gen": chains[k][0](), "ready": 0.0, "done": False, "pending": []}
                    pull(ch)
                    state[k] = ch
            best, bt, bk = None, None, None
            for k in range(n):
                ch = state[k]
                if not isinstance(ch, dict):
                    continue
                if ch["done"] and not ch["pending"]:
                    state[k] = "done"
                    ndone += 1
                    continue
                cand = max(ch["ready"], S.eng_free[ch["pending"][0][1]]) if ch["pending"] else ch["ready"]
                if ch["pending"]:
                    a0 = ch["pending"][0][7]
                    if a0 is not None and S.act_set not in S.ASETS[a0]:
                        cand += 1.3
                if bt is None or cand < bt - TOL:
                    best, bt, bk = ch, cand, k
            if best is None:
                continue
            for (kind, e, fn, r, w, inc, t, aset) in best["pending"]:
                st_ = max(best["ready"], S.eng_free[e])
                if aset is not None and S.act_set not in S.ASETS[aset]:
                    S.act_set = S.ASETS[aset][0]
                    st_ += 1.3
                S.eng_free[e] = st_ + t
                best["ready"] = st_ + t + LAT
                if kind == "op":
                    S.op(e, fn, r=r, w=w, inc=inc)
                else:
                    sname, out, in_, kw = fn
                    S.dma(e, sname, out, in_, r=r, w=w, **kw)
            best["pending"] = []
            if not best["done"]:
                pull(best)

    S.dma("sp", "cst", cst[:], cst_d[:, :], w=["cst"])
    S.dma("sp", "ident", ident[:], ident_d[:, :], w=["ident"])
    for tt in range(4):
        S.dma("sp", "xin%d" % tt, x1[:, tt, :], x_d[tt, :, :], w=["x1_%d" % tt])
    S.dma("sp", "bc", bc[:], bc_d[:, :], w=["bc"])
    S.op("dve", lambda e: e.memset(wr_bd[:], 0.0), w=["wr_bd"])
    S.op("dve", lambda e: e.memset(wi_bd[:], 0.0), w=["wi_bd"])
    S.op("dve", lambda e: e.memset(wsP[:], 0.0), w=["wsP"])
    S.op("dve", lambda e: e.memset(ones_r[:], 1.0), w=["ones_r"])
    S.op("dve", lambda e: e.memset(ones_c[:], 1.0), w=["ones_c"])
    S.op("dve", lambda e: e.memset(osm[:], 0.0), w=["osm"])
    S.op("dve", lambda e: e.memset(hn2T[:, :, 0:2], 0.0), w=["hn2T_h"])

    def small_weight_dmas():
        for hh in range(8):
            ct, o = hh // 2, (hh % 2) * 64
            S.dma("pool", "wsm", wr_bd[o:o + 64, ct, o:o + 64], wr_d[hh], r=["wr_bd"])
            S.dma("pool", "wsm", wi_bd[o:o + 64, ct, o:o + 64], wi_d[hh], r=["wi_bd"])
        S.dma("pool", "wsm", wsP[0:64, :, :], wst_d[:, 0:64, :].rearrange("h j i -> j h i"), r=["wsP"])
        S.dma("pool", "wsm", wsP[64:128, :, 64:128], wst_d[:, 64:128, 64:128].rearrange("h j i -> j h i"), r=["wsP"])
        S.dma("pool", "wsm", wsS[:, :, :], wst_d[:, 0:64, 0:64].rearrange("h j i -> j h i"))
        S.dma("pool", "wsm", bsr[:, :, :], bs_d[:, :, :])
        S.dma("pool", "wsm", rowb[0:1, 1536:2048], bs_d[0:1, :, :].rearrange("o h i -> o (h i)"))
        S.group_written("wsm", ["wr_bd", "wi_bd", "wsP", "wsS", "bsr", "rowb_bs"])

    S.op("act", lambda e: e.activation(out=smt[:, 0:4], in_=C("lam", 4), func=AF.Exp, scale=-1.0), r=["cst"], w=["smt"])
    S.op("act", lambda e: e.activation(out=smt[:, 4:8], in_=smt[:, 0:4], func=AF.Ln, bias=1.0), r=["smt"], w=["smt"])
    S.op("dve", lambda e: e.tensor_scalar(out=clam[:, 0:4], in0=smt[:, 4:8], scalar1=-4.0, scalar2=None, op0=ALU.mult),
         r=["smt"], w=["clam"])
    S.op("dve", lambda e: e.tensor_scalar(out=clam[:, 4:8], in0=smt[:, 4:8], scalar1=-8.0, scalar2=None, op0=ALU.mult),
         r=["smt"], w=["clam"])
    S.op("dve", lambda e: e.tensor_scalar(out=clam[:, 8:12], in0=C("br", 4), scalar1=0.5, scalar2=None, op0=ALU.mult),
         r=["cst"], w=["clam"])
    S.op("dve", lambda e: e.tensor_scalar(out=clam[:, 12:16], in0=C("bi", 4), scalar1=0.5, scalar2=None, op0=ALU.mult),
         r=["cst"], w=["clam"])
    S.op("dve", lambda e: e.tensor_tensor(out=smt[:, 8:12], in0=C("goa", 4), in1=C("goa", 4), op=ALU.mult),
         r=["cst"], w=["smt"])
    S.op("dve", lambda e: e.tensor_tensor(out=smt[:, 12:16], in0=C("gob", 4), in1=C("gob", 4), op=ALU.mult),
         r=["cst"], w=["smt"])
    S.op("dve", lambda e: e.reciprocal(out=smt[:, 0:8], in_=smt[:, 8:16]), r=["smt"], w=["smt"])
    S.op("dve", lambda e: e.tensor_copy(out=invg2[:, :], in_=smt[:, 0:8]), r=["smt"], w=["invg2"])
    S.op("dve", lambda e: e.tensor_tensor(out=gvgob[:, :], in0=C("gv", 4), in1=C("gob", 4), op=ALU.mult),
         r=["cst"], w=["gvgob"])
    S.op("dve", lambda e: e.reciprocal(out=igf[0:1, :], in_=bc[0:1, BC_GV:BC_GV + 512]), r=["bc"], w=["igf"])
    S.op("dve", lambda e: e.tensor_copy(out=rowb[0:1, 0:512], in_=igf[0:1, :]), r=["igf"], w=["rowb_g"])
    S.op("dve", lambda e: e.tensor_tensor(out=rowb[0:1, 512:1024], in0=bc[0:1, BC_BV:BC_BV + 512], in1=igf[0:1, :],
                                          op=ALU.mult), r=["bc", "igf"], w=["rowb_b"])

    def colsum_setup():
        for hd in range(4):
            S.op("pe", lambda e, hd=hd: e.matmul(B[7][0:1, hd * 128:(hd + 1) * 128], lhsT=ones_c[:, 0:1],
                                                rhs=wsP[:, hd, :], start=True, stop=True),
                 r=["ones_c", "wsP"], w=["B7"], inc=(hd == 3))
        S.op("dve", lambda e: e.tensor_copy(out=rowb[0:1, 1024:1536], in_=B[7][0:1, 0:512]), r=["B7"], w=["rowb_cs"])

    smc = [sb("sm%d" % ci, [128, 32]) for ci in range(3)]
    smg = sb("smg", [128, 32])
    SS2, RAB = smg[:, 0:8], smg[:, 8:16]

    def SM(ci, name):
        lay = {"ssq": (0, 1), "rs": (1, 2), "mv": (2, 4), "st6": (4, 10), "rv": (10, 11), "nmr": (11, 12),
               "ssm": (12, 13), "rm": (13, 14), "ssf": (14, 16), "rf": (16, 17)}
        a, b = lay[name]
        return smc[ci][:, a:b], name + str(ci)

    def rstd(dst, src, scale, rows, keys_r, key_w):
        S.op("act", lambda e: e.activation(out=dst[:rows], in_=src[:rows], func=AF.Sqrt, scale=scale, bias=EPS),
             r=keys_r, w=[key_w], t=0.25, aset="s")
        yield
        S.op("dve", lambda e: e.reciprocal(out=dst[:rows], in_=dst[:rows]), r=[key_w], w=[key_w], t=0.18)
        yield

    def ch_norm_transpose(tt, rows, ci, dstT, col0, gname, dkey):
        xt = x1[:rows, tt, :]
        xk = "x1_%d" % tt
        T = TP[ci]
        tb = (6, 7) if ci == 1 else (0, 1)
        ssq, kssq = SM(ci, "ssq")
        rs, krs = SM(ci, "rs")
        S.op("act", lambda e: e.activation(out=T["junk"][:rows, :], in_=xt, func=AF.Square, accum_out=ssq[:rows]),
             r=[xk], w=["junk%d" % ci, kssq])
        yield
        yield from rstd(rs, ssq, 1.0 / D, rows, [kssq], krs)
        S.op("act", lambda e: e.activation(out=T["xn"][:rows, :], in_=xt, func=AF.Copy, scale=rs[:rows]),
             r=[xk, krs], w=["xn%d" % ci], t=1.1)
        yield
        for kt in range(8):
            bk = B[tb[kt // 4]]
            c = (kt % 4) * 128
            S.op("pe", lambda e, kt=kt, bk=bk, c=c: e.transpose(out=bk[:, c:c + rows],
                                                                 in_=T["xn"][:rows, kt * 128:(kt + 1) * 128],
                                                                 identity=ident[:rows, :rows]),
                 r=["xn%d" % ci, "ident"], w=["B%d" % tb[kt // 4]], inc=(kt % 4 == 3))
        go = _off[gname]
        for hb in range(2):
            bk = B[tb[hb]]
            gsl = cst[:, go + 4 * hb:go + 4 * hb + 4].unsqueeze(2).to_broadcast([128, 4, 128])
            S.op("dve", lambda e, bk=bk, gsl=gsl, hb=hb: e.tensor_tensor(
                out=dstT[:, 4 * hb:4 * hb + 4, col0:col0 + rows],
                in0=bk[:, 0:512].rearrange("p (a b) -> p a b", a=4)[:, :, 0:rows],
                in1=gsl[:, :, 0:rows], op=ALU.mult),
                 r=["B%d" % tb[hb], "cst"], w=[dkey])
        yield

    def phase_A(h):
        for tt in range(NTT):
            if h == 0 and tt < 4:
                continue
            rows = 64 if tt == 8 else 128
            src = x_d[16, h * 64:(h + 1) * 64, :] if tt == 8 else x_d[h * 8 + tt, :, :]
            S.dma("sp", "xin%d" % tt, x1[:rows, tt, :], src, w=["x1_%d" % tt])
        for st in range(3):
            isS = (st == 2)
            N = NSH if isS else 512
            nsub = 1 if isS else 4
            rows = 64 if isS else 128
            seq = (1 + h) if isS else 0
            tts = [8] if isS else [st * 4 + s for s in range(nsub)]
            HNK = ["hnT%d" % s for s in range(nsub)]
            if st == 0:
                run_window([(lambda s=s: ch_norm_transpose(tts[s], rows, s % 2, hnT, s * 128, "gpre1", "hnT%d" % s))
                            for s in range(nsub)])
            S.fence(S2_KEYS, TP_KEYS)

            def chz(ct):
                zb = (2, 3, 4)
                for (bk, c0) in ((zb[0], ct * 128), (zb[1], 512 + ct * 128), (zb[2], 1024 + ct * 128)):
                    for kt in range(8):
                        S.op("pe", lambda e, bk=bk, c0=c0, kt=kt: e.matmul(B[bk][:, 0:N], lhsT=win_sb[:, kt, c0:c0 + 128],
                                                                          rhs=hnT[:, kt, 0:N], start=(kt == 0),
                                                                          stop=(kt == 7)),
                             r=["win%d" % kt] + HNK, w=["B%d" % bk], inc=(kt == 7))
                yield
                S.op("act", lambda e: e.activation(out=gg4[:, ct, 0:N], in_=B[zb[0]][:, 0:N], func=AF.Gelu_apprx_tanh),
                     r=["B%d" % zb[0]], w=["gg%d" % ct], aset="g")
                yield
                S.op("act", lambda e: e.activation(out=xa4[:, ct, 3:3 + N], in_=B[zb[1]][:, 0:N], func=AF.Copy),
                     r=["B%d" % zb[1]], w=["xa%d" % ct])
                yield
                S.op("act", lambda e: e.activation(out=u_sb[:, ct, 0:N], in_=B[zb[2]][:, 0:N], func=AF.Copy),
                     r=["B%d" % zb[2]], w=["u_sb"])
                yield

            def chain_bufs(ct, ci):
                if not isS:
                    return S2[ci], 0
                off = 128 * (ct // 2)
                return {k: v[:, off:off + 128] for k, v in S2[ct % 2].items()}, off

            def ch2a(ct, ci):
                Q, boff = chain_bufs(ct, ci)
                zb = (5, 6) if ct % 2 == 0 else (7, 1)
                bR, bI = B[zb[0]][:, boff:], B[zb[1]][:, boff:]
                xa = xa4[:, ct, :]
                kxa = "xa%d" % ct
                if isS:
                    o = _off["lconv"] + (ct * 2 + h) * 3
                    hsrc, hk = cst[:, o:o + 3], "cst"
                else:
                    o = OS_CONV + (ct * 3 + 0) * 3
                    hsrc, hk = osm[:, o:o + 3], "osm"
                S.op("dve", lambda e: e.tensor_copy(out=xa[:, 0:3], in_=hsrc), r=[hk], w=[kxa], t=0.1)
                yield
                wc = lambda j: C("wca", 1, ct * 4 + j)
                S.op("dve", lambda e: e.tensor_scalar(out=Q["xc"][:, 0:N], in0=xa[:, 0:N], scalar1=wc(0),
                                                     scalar2=C("bca", 1, ct), op0=ALU.mult, op1=ALU.add),
                     r=[kxa, "cst"], w=["xc%d" % ci])
                yield
                for j in (1, 2, 3):
                    S.op("dve", lambda e, j=j: e.scalar_tensor_tensor(out=Q["xc"][:, 0:N], in0=xa[:, j:j + N],
                                                                     scalar=wc(j), in1=Q["xc"][:, 0:N], op0=ALU.mult,
                                                                     op1=ALU.add),
                         r=[kxa, "xc%d" % ci, "cst"], w=["xc%d" % ci])
                    yield
                o2 = OS_CONV + (ct * 3 + seq) * 3
                S.op("dve", lambda e: e.tensor_copy(out=osm[:, o2:o2 + 3], in_=xa[:, N:N + 3]), r=[kxa], w=["osm"], t=0.1)
                yield
                S.op("act", lambda e: e.activation(out=Q["xcb"][:, 0:N], in_=Q["xc"][:, 0:N], func=AF.Copy),
                     r=["xc%d" % ci], w=["xcb%d" % ci])
                yield
                S.op("pe", lambda e: e.matmul(bR[:, 0:N], lhsT=wr_bd[:, ct, :], rhs=Q["xcb"][:, 0:N], start=True,
                                              stop=True), r=["wr_bd", "xcb%d" % ci], w=["B%d" % zb[0]])
                S.op("pe", lambda e: e.matmul(bI[:, 0:N], lhsT=wi_bd[:, ct, :], rhs=Q["xcb"][:, 0:N], start=True,
                                              stop=True), r=["wi_bd", "xcb%d" % ci], w=["B%d" % zb[1]])
                yield
                S.op("act", lambda e: e.activation(out=Q["rr"][:, 0:N], in_=bR[:, 0:N], func=AF.Tanh, scale=0.5,
                                                   bias=clam[:, 8 + ct:9 + ct]), r=["B%d" % zb[0], "clam"], w=["rr%d" % ci], aset="t")
                yield
                S.op("act", lambda e: e.activation(out=Q["ii"][:, 0:N], in_=bI[:, 0:N], func=AF.Tanh, scale=0.5,
                                                   bias=clam[:, 12 + ct:13 + ct]), r=["B%d" % zb[1], "clam"], w=["ii%d" % ci], aset="t")
                yield

                yield from ch2b(ct, ci)

            def ch2b(ct, ci):
                Q, boff = chain_bufs(ct, ci)
                S.op("act", lambda e: e.activation(out=Q["aa"][:, 0:N], in_=Q["rr"][:, 0:N], func=AF.Exp,
                                                   scale=clam[:, ct:ct + 1], bias=clam[:, ct:ct + 1]),
                     r=["rr%d" % ci, "clam"], w=["aa%d" % ci], aset="e")
                yield
                S.op("act", lambda e: e.activation(out=Q["rr"][:, 0:N], in_=Q["rr"][:, 0:N], func=AF.Exp,
                                                   scale=clam[:, 4 + ct:5 + ct], bias=clam[:, 4 + ct:5 + ct]),
                     r=["rr%d" % ci, "clam"], w=["rr%d" % ci], aset="e")
                yield
                S.op("dve", lambda e: e.scalar_tensor_tensor(out=Q["ii"][:, 0:N], in0=Q["ii"][:, 0:N], scalar=1.0,
                                                            in1=Q["xc"][:, 0:N], op0=ALU.add, op1=ALU.mult),
                     r=["ii%d" % ci, "xc%d" % ci], w=["ii%d" % ci])
                yield
                S.op("act", lambda e: e.activation(out=Q["rr"][:, 0:N], in_=Q["rr"][:, 0:N], func=AF.Sqrt, scale=-1.0,
                                                   bias=1.0), r=["rr%d" % ci], w=["rr%d" % ci], aset="s")
                yield
                if (not isS) and h == 0 and st == 0:
                    S.op("dve", lambda e: e.memset(Q["rr"][:, 0:1], 1.0), w=["rr%d" % ci], t=0.08)
                    yield
                S.op("dve", lambda e: e.scalar_tensor_tensor(out=Q["rr"][:, 0:N], in0=Q["rr"][:, 0:N], scalar=0.5,
                                                            in1=Q["ii"][:, 0:N], op0=ALU.mult, op1=ALU.mult),
                     r=["rr%d" % ci, "ii%d" % ci], w=["rr%d" % ci])
                yield
                if isS:
                    o = _off["h0"] + ct * 2 + h
                    init, ik = cst[:, o:o + 1], "cst"
                else:
                    init, ik = osm[:, OS_H + ct * 3:OS_H + ct * 3 + 1], "osm"
                S.op("dve", lambda e: e.tensor_tensor_scan(out=Q["hbuf"][:, 0:N], data0=Q["aa"][:, 0:N],
                                                          data1=Q["rr"][:, 0:N], initial=init, op0=ALU.mult,
                                                          op1=ALU.add),
                     r=["aa%d" % ci, "rr%d" % ci, ik], w=["hbuf%d" % ci])
                yield
                o3 = OS_H + ct * 3 + seq
                S.op("dve", lambda e: e.tensor_copy(out=osm[:, o3:o3 + 1], in_=Q["hbuf"][:, N - 1:N]),
                     r=["hbuf%d" % ci], w=["osm"], t=0.15)
                yield
                S.op("dve", lambda e: e.scalar_tensor_tensor(out=yT[:, ct, 0:N], in0=Q["hbuf"][:, 0:N],
                                                            scalar=C("goa", 1, ct), in1=gg4[:, ct, 0:N],
                                                            op0=ALU.mult, op1=ALU.mult),
                     r=["hbuf%d" % ci, "gg%d" % ct, "cst"], w=["yTa%d" % ct])
                yield
                S.op("dve", lambda e: e.tensor_tensor(out=Q["sqa"][:, 0:N], in0=yT[:, ct, 0:N], in1=yT[:, ct, 0:N],
                                                     op=ALU.mult), r=["yTa%d" % ct], w=["sqa%d" % ci], t=0.4)
                yield
                for s in range(nsub):
                    S.op("pe", lambda e, s=s: e.matmul(B[0][:rows, s * 8 + ct:s * 8 + ct + 1],
                                                      lhsT=Q["sqa"][:, s * 128:s * 128 + rows],
                                                      rhs=invg2[:, ct:ct + 1], start=True, stop=True),
                         r=["sqa%d" % ci, "invg2"], w=["B0"], inc=(s == nsub - 1))
                yield

            dag = []
            for ct in range(4):
                dag.append(((lambda ct=ct: chz(ct)), [2 * ct - 2] if ct > 0 else []))
                if isS:
                    dag.append(((lambda ct=ct: ch2a(ct, ct)), [2 * ct]))
                else:
                    dag.append(((lambda ct=ct: ch2a(ct, ct % 2)), [2 * ct] + ([2 * ct - 3] if ct >= 2 else [])))
            run_dag(dag)
            S.fence(TP_KEYS, S2_KEYS)
            if h == 0 and st == 0:
                colsum_setup()

            def ch3(s, ci):
                c0 = s * 128
                T = TP[ci]
                vb, sbk = (2, 3) if ci == 0 else (4, 5)
                kv, ks = "B%d" % vb, "B%d" % sbk
                for kt in range(8):
                    S.op("pe", lambda e, kt=kt: e.matmul(B[vb][:rows, :], lhsT=hnT[:, kt, c0:c0 + rows],
                                                        rhs=win_sb[:, kt, 1536:2048], start=(kt == 0), stop=(kt == 7)),
                         r=["hnT%d" % s, "win%d" % kt], w=[kv], inc=(kt == 7))
                yield
                st6, kst6 = SM(ci, "st6")
                mv, kmv = SM(ci, "mv")
                rv, krv = SM(ci, "rv")
                nmr, knmr = SM(ci, "nmr")
                S.op("dve", lambda e: e.bn_stats(out=st6[:rows], in_=B[vb][:rows, :]), r=[kv], w=[kst6])
                yield
                S.op("dve", lambda e: e.bn_aggr(out=mv[:rows], in_=st6[:rows]), r=[kst6], w=[kmv], t=0.18)
                yield
                yield from rstd(rv, mv[:, 1:2], 1.0, rows, [kmv], krv)
                S.op("dve", lambda e: e.tensor_scalar(out=nmr[:rows], in0=mv[:rows, 0:1], scalar1=rv[:rows],
                                                     scalar2=-1.0, op0=ALU.mult, op1=ALU.mult),
                     r=[kmv, krv], w=[knmr], t=0.2)
                yield
                if isS:
                    S.op("act", lambda e: e.activation(out=T["vn"][:rows, :], in_=B[vb][:rows, :], func=AF.Identity,
                                                       scale=rv[:rows], bias=nmr[:rows]),
                         r=[kv, krv, knmr], w=["vn%d" % ci])
                    yield
                    S.op("dve", lambda e: e.tensor_tensor(out=T["vn"][:rows, :], in0=T["vn"][:rows, :],
                                                         in1=bc[:rows, BC_GV:BC_GV + 512], op=ALU.mult),
                         r=["vn%d" % ci, "bc"], w=["vn%d" % ci])
                    yield
                    S.op("dve", lambda e: e.tensor_tensor(out=T["vn"][:rows, :], in0=T["vn"][:rows, :],
                                                         in1=bc[:rows, BC_BV:BC_BV + 512], op=ALU.add),
                         r=["vn%d" % ci, "bc"], w=["vn%d" % ci])
                    yield
                    S.op("act", lambda e: e.activation(out=T["vnb"][:rows, :], in_=T["vn"][:rows, :], func=AF.Copy),
                         r=["vn%d" % ci], w=["vnb%d" % ci])
                    S.dma("sp", "vns", vns_d[h, :, :], T["vn"][:rows, :], r=["vn%d" % ci])
                    yield
                    for hd in range(4):
                        S.op("pe", lambda e, hd=hd: e.matmul(B[sbk][:, hd * 128:hd * 128 + rows],
                                                            lhsT=T["vnb"][:rows, hd * 128:(hd + 1) * 128],
                                                            rhs=wsS[:rows, hd, 0:rows], start=True, stop=False),
                             r=["vnb%d" % ci, "wsS"], w=[ks], inc=False)
                        S.op("pe", lambda e, hd=hd: e.matmul(B[sbk][:, hd * 128:hd * 128 + rows], lhsT=ones_r[0:1, :],
                                                            rhs=bsr[0:1, hd, 0:rows], start=False, stop=True),
                             r=["ones_r", "bsr"], w=[ks], inc=(hd == 3))
                    yield
                    gsc = lambda hd: C("gob", 1, hd)
                else:
                    S.op("act", lambda e: e.activation(out=T["vnb"][:rows, :], in_=B[vb][:rows, :], func=AF.Identity,
                                                       scale=rv[:rows], bias=nmr[:rows]),
                         r=[kv, krv, knmr], w=["vnb%d" % ci])
                    yield
                    for hd in range(4):
                        S.op("pe", lambda e, hd=hd: e.matmul(B[sbk][:, hd * 128:(hd + 1) * 128],
                                                            lhsT=T["vnb"][:, hd * 128:(hd + 1) * 128],
                                                            rhs=wsP[:, hd, :], start=True, stop=False),
                             r=["vnb%d" % ci, "wsP"], w=[ks], inc=False)
                        S.op("pe", lambda e, hd=hd: e.matmul(B[sbk][:, hd * 128:(hd + 1) * 128],
                                                            lhsT=rowb[0:1, 512 + hd * 128:512 + (hd + 1) * 128],
                                                            rhs=rowb[0:1, 1024 + hd * 128:1024 + (hd + 1) * 128],
                                                            start=False, stop=False),
                             r=["rowb_b", "rowb_cs"], w=[ks], inc=False)
                        S.op("pe", lambda e, hd=hd: e.matmul(B[sbk][:, hd * 128:(hd + 1) * 128],
                                                            lhsT=rowb[0:1, hd * 128:(hd + 1) * 128],
                                                            rhs=rowb[0:1, 1536 + hd * 128:1536 + (hd + 1) * 128],
                                                            start=False, stop=True),
                             r=["rowb_g", "rowb_bs"], w=[ks], inc=(hd == 3))
                    yield
                    gsc = lambda hd: gvgob[:, hd:hd + 1]
                for hd in range(4):
                    S.op("dve", lambda e, hd=hd: e.scalar_tensor_tensor(
                        out=yT[:, 4 + hd, c0:c0 + rows], in0=B[sbk][:, hd * 128:hd * 128 + rows],
                        scalar=gsc(hd), in1=u_sb[:, hd, c0:c0 + rows], op0=ALU.mult, op1=ALU.mult),
                         r=[ks, "u_sb", "cst", "gvgob"], w=["yTb%d" % s])
                    yield
                S.op("dve", lambda e: e.tensor_tensor(out=T["sqb"][:, :, 0:rows], in0=yT[:, 4:8, c0:c0 + rows],
                                                     in1=yT[:, 4:8, c0:c0 + rows], op=ALU.mult),
                     r=["yTb%d" % s], w=["sqb%d" % ci], t=0.4)
                yield
                for hd in range(4):
                    S.op("pe", lambda e, hd=hd: e.matmul(B[0][:rows, s * 8 + 4 + hd:s * 8 + 5 + hd],
                                                        lhsT=T["sqb"][:, hd, 0:rows], rhs=invg2[:, 4 + hd:5 + hd],
                                                        start=True, stop=True),
                         r=["sqb%d" % ci, "invg2"], w=["B0"], inc=(hd == 3))
                yield

            run_window([(lambda s=s: ch3(s, s % 2)) for s in range(nsub)], lag=7)

            S.op("dve", lambda e: e.tensor_reduce(out=SS2[:rows, 0:2 * nsub],
                                                 in_=B[0][:rows, 0:8 * nsub].rearrange("p (a b) -> p a b", b=4),
                                                 axis=AX.X, op=ALU.add), r=["B0"], w=["ss2"])
            for _ in rstd(RAB[:, 0:2 * nsub], SS2[:, 0:2 * nsub], 1.0 / DA, rows, ["ss2"], "rab"):
                pass
            YK = ["yTa%d" % c for c in range(4)] + ["yTb%d" % s for s in range(nsub)]

            def ch4(s, ci):
                tt = tts[s]
                c0 = s * 128
                xk = "x1_%d" % tt
                T = TP[ci]
                pa, pb = (2, 3) if ci == 0 else (4, 5)
                for nh in range(2):
                    for grp, bk in ((0, pa), (1, pb)):
                        for k in range(4):
                            kt = grp * 4 + k
                            S.op("pe", lambda e, kt=kt, bk=bk, nh=nh, k=k: e.matmul(
                                B[bk][:rows, :], lhsT=yT[:, kt, c0:c0 + rows], rhs=wo_sb[:, kt, nh * 512:(nh + 1) * 512],
                                start=(k == 0), stop=(k == 3)), r=YK + ["wo%d" % (kt // 2)], w=["B%d" % bk],
                                 inc=(k == 3))
                    yield
                    S.op("act", lambda e, nh=nh: e.activation(out=T["tmpA"][:rows, :], in_=B[pa][:rows, :], func=AF.Copy,
                                                             scale=RAB[:rows, 2 * s:2 * s + 1]),
                         r=["B%d" % pa, "rab"], w=["tmpA%d" % ci])
                    yield
                    S.op("dve", lambda e, nh=nh: e.scalar_tensor_tensor(
                        out=T["mix"][:rows, nh * 512:(nh + 1) * 512], in0=B[pb][:rows, :],
                        scalar=RAB[:rows, 2 * s + 1:2 * s + 2], in1=T["tmpA"][:rows, :],
                        op0=ALU.mult, op1=ALU.add), r=["B%d" % pb, "tmpA%d" % ci, "rab"], w=["mix%d" % ci])
                    yield
                ssm, kssm = SM(ci, "ssm")
                rm, krm = SM(ci, "rm")
                S.op("act", lambda e: e.activation(out=T["junk"][:rows, :], in_=T["mix"][:rows, :], func=AF.Square,
                                                   accum_out=ssm[:rows]), r=["mix%d" % ci], w=["junk%d" % ci, kssm])
                yield
                yield from rstd(rm, ssm, 1.0 / D, rows, [kssm], krm)
                S.op("dve", lambda e: e.scalar_tensor_tensor(out=T["mix"][:rows, :], in0=T["mix"][:rows, :],
                                                            scalar=rm[:rows], in1=bc[:rows, BC_GP1:BC_GP1 + D],
                                                            op0=ALU.mult, op1=ALU.mult),
                     r=["mix%d" % ci, krm, "bc"], w=["mix%d" % ci])
                yield
                S.op("dve", lambda e: e.tensor_tensor(out=x1[:rows, tt, :], in0=T["mix"][:rows, :],
                                                     in1=x1[:rows, tt, :], op=ALU.add),
                     r=["mix%d" % ci, xk], w=[xk])
                yield
                col0 = (2 + NPH) if isS else (2 + tt * 128)
                yield from ch_norm_transpose(tt, rows, ci, hn2T, col0, "gpre2", "hn2T_%d" % tt)

            dag = [((lambda s=s: ch4(s, s % 2)), [s - 2] if s >= 2 else []) for s in range(nsub)]
            if st < 2:
                nS = (st + 1 == 2)
                n_rows = 64 if nS else 128
                n_tts = [8] if nS else [(st + 1) * 4 + s for s in range(4)]
                base = len(dag)
                for s2, tt2 in enumerate(n_tts):
                    dag.append(((lambda s2=s2, tt2=tt2: ch_norm_transpose(tt2, n_rows, 2, hnT, s2 * 128, "gpre1",
                                                                         "hnT%d" % s2)),
                                [base + s2 - 1] if s2 > 0 else []))
            run_dag(dag)

    def phase_B(h):
        n_last = 341
        tiles = [(0, 342, False), (342, 341, False), (683, n_last, True)]

        def wup_dma(j):
            slot = j % 3
            S.dma("pool", "wup%d" % slot, wup[slot][:, :, :].rearrange("p a b -> p (a b)"), wup_d[j, :, :],
                  w=["wup%d" % slot])

        wup_dma(0)
        wup_dma(1)
        tcount = 0
        for j in range(24):
            slot = j % 3
            if j + 2 < 24:
                wup_dma(j + 2)
            if j % 2 == 1:
                jd = j // 2
                S.dma("pool", "wdn%d" % jd, wdn_sb[:, 2 * jd:2 * jd + 2, :], wdn_d[:, 2 * jd:2 * jd + 2, :],
                      w=["wdn%d" % jd])
            cg, cl = j, 24 + j
            for (t0, n, withS) in tiles:
                par = tcount % NSET
                tcount += 1
                bg, bl = B[2 * par], B[2 * par + 1]
                kg, kl = "B%d" % (2 * par), "B%d" % (2 * par + 1)
                ag, al = accg[par], accl[par]
                kag, kal = "accg%d" % par, "accl%d" % par
                nq = n + 2 + NSH if withS else n
                if withS:
                    o = _off["ffnst"]
                    for (bk, kk, cc) in ((bg, kg, cg), (bl, kl, cl)):
                        src = cst[:, o + (cc * 2 + h) * 2:o + (cc * 2 + h) * 2 + 2]
                        S.op("pe", lambda e, bk=bk, src=src: e.matmul(bk[:, n + 2:n + 4], lhsT=ident[:, :], rhs=src,
                                                                     start=True, stop=True),
                             r=["cst", "ident"], w=[kk], inc=False)
                for (bk, kk, half) in ((bg, kg, 0), (bl, kl, 1)):
                    for kt in range(8):
                        S.op("pe", lambda e, bk=bk, kt=kt, half=half: e.matmul(
                            bk[:, 0:n + 2], lhsT=wup[slot][:, kt, half * 128:(half + 1) * 128],
                            rhs=hn2T[:, kt, t0:t0 + n + 2], start=(kt == 0), stop=(kt == 7)),
                             r=["wup%d" % slot] + HN2_KEYS, w=[kk], inc=(kt == 7))
                    if withS:
                        for kt in range(8):
                            S.op("pe", lambda e, bk=bk, kt=kt, half=half: e.matmul(
                                bk[:, n + 4:n + 4 + NSH], lhsT=wup[slot][:, kt, half * 128:(half + 1) * 128],
                                rhs=hn2T[:, kt, 2 + NPH:2 + NPH + NSH], start=(kt == 0), stop=(kt == 7)),
                                 r=["wup%d" % slot] + HN2_KEYS, w=[kk], inc=(kt == 7))
                branches = ((bg, kg, ag, kag, cg), (bl, kl, al, kal, cl))
                w3 = lambda tap, cc: C("wcf", 1, cc * 3 + tap)
                for (bk, kk, acc, ka, cc) in branches:
                    S.op("act", lambda e, bk=bk, acc=acc, cc=cc: e.activation(out=acc[:, 0:nq], in_=bk[:, 0:nq],
                                                                            func=AF.Identity, scale=w3(0, cc),
                                                                            bias=C("bcf", 1, cc)),
                         r=[kk, "cst"], w=[ka])
                for tap in (1, 2):
                    for (bk, kk, acc, ka, cc) in branches:
                        S.op("dve", lambda e, bk=bk, acc=acc, tap=tap, cc=cc: e.scalar_tensor_tensor(
                            out=acc[:, 0:nq], in0=bk[:, tap:tap + nq], scalar=w3(tap, cc), in1=acc[:, 0:nq],
                            op0=ALU.mult, op1=ALU.add), r=[kk, ka, "cst"], w=[ka])
                for (bk, kk, acc, ka, cc) in branches:
                    if withS:
                        oo = OS_FFN + (cc * 3 + 1 + h) * 2
                        S.op("dve", lambda e, bk=bk, oo=oo: e.tensor_copy(out=osm[:, oo:oo + 2], in_=bk[:, n + 66:n + 68]),
                             r=[kk], w=["osmS%d" % cc])
                        if h == 1:
                            oo = OS_FFN + (cc * 3 + 0) * 2
                            S.op("dve", lambda e, bk=bk, oo=oo: e.tensor_copy(out=osm[:, oo:oo + 2], in_=bk[:, n:n + 2]),
                                 r=[kk], w=["osmP%d" % cc])
                S.op("act", lambda e, ag=ag: e.activation(out=ag[:, 0:nq], in_=ag[:, 0:nq], func=AF.Gelu_apprx_tanh),
                     r=[kag], w=[kag])
                S.op("pool", lambda e, ag=ag, al=al: e.tensor_tensor(out=gT[:, j, t0:t0 + nq], in0=ag[:, 0:nq],
                                                                    in1=al[:, 0:nq], op=ALU.mult),
                     r=[kag, kal], w=["gT%d" % j])

    def phase_C(h):
        for tt in range(NTT):
            rows = 64 if tt == 8 else 128
            c0 = (NPH + 2) if tt == 8 else tt * 128
            par = tt % 2
            xk = "x1_%d" % tt
            ssf, kssf = SM(par, "ssf")
            rf, krf = SM(par, "rf")
            for nh in range(2):
                bk = 4 + 2 * par + nh
                for c in range(24):
                    S.op("pe", lambda e, bk=bk, c=c, nh=nh: e.matmul(B[bk][:rows, :], lhsT=gT[:, c, c0:c0 + rows],
                                                                    rhs=wdn_sb[:, c, nh * 512:(nh + 1) * 512],
                                                                    start=(c == 0), stop=(c == 23)),
                         r=["gT%d" % c, "wdn%d" % (c // 2)], w=["B%d" % bk], inc=(c == 23))
            for nh in range(2):
                bk = 4 + 2 * par + nh
                S.op("act", lambda e, bk=bk, nh=nh: e.activation(out=junk2[:rows, :], in_=B[bk][:rows, :],
                                                                func=AF.Square, accum_out=ssf[:rows, nh:nh + 1]),
                     r=["B%d" % bk], w=["hn2T_0", "hn2T_1", "hn2T_2", "hn2T_3", kssf])
            S.op("dve", lambda e: e.tensor_tensor(out=rf[:rows], in0=ssf[:rows, 0:1], in1=ssf[:rows, 1:2], op=ALU.add),
                 r=[kssf], w=[krf])
            for _ in rstd(rf, rf, 1.0 / D, rows, [krf], krf):
                pass
            t2h = (accg[par], accl[par])
            t2k = ("accg%d" % par, "accl%d" % par)
            for nh in range(2):
                bk = 4 + 2 * par + nh
                S.op("dve", lambda e, bk=bk, nh=nh: e.scalar_tensor_tensor(
                    out=t2h[nh][:rows, :], in0=B[bk][:rows, :], scalar=rf[:rows],
                    in1=bc[:rows, BC_GP2 + nh * 512:BC_GP2 + (nh + 1) * 512], op0=ALU.mult, op1=ALU.mult),
                     r=["B%d" % bk, krf, "bc"], w=[t2k[nh]])
                S.op("dve", lambda e, nh=nh: e.tensor_tensor(out=x1[:rows, tt, nh * 512:(nh + 1) * 512],
                                                             in0=t2h[nh][:rows, :],
                                                             in1=x1[:rows, tt, nh * 512:(nh + 1) * 512], op=ALU.add),
                     r=[t2k[nh], xk], w=[xk])
            dst = y_d[16, h * 64:(h + 1) * 64, :] if tt == 8 else y_d[h * 8 + tt, :, :]
            S.dma("sp", "yout%d" % tt, dst, x1[:rows, tt, :], r=[xk])

    for h in range(2):
        if h == 1:
            S.op("dve", lambda e: e.tensor_copy(out=hn2T[:, :, 0:2], in_=hn2T[:, :, NPH:NPH + 2]),
                 r=HN2_KEYS, w=["hn2T_h"])
        S.fence(WA_KEYS, WD_KEYS)
        for kt in range(8):
            S.dma("pool", "win%d" % kt, win_sb[:, kt, :], win_d[:, kt, :],
                  r=(["x1_3"] if (h == 0 and kt == 0) else []), w=["win%d" % kt])
        for k2 in range(4):
            S.dma("pool", "wo%d" % k2, wo_sb[:, 2 * k2:2 * k2 + 2, :], wo_d[:, 2 * k2:2 * k2 + 2, :], w=["wo%d" % k2])
        if h == 0:
            small_weight_dmas()
        S.fence(ARENA_KEYS_A, ARENA_KEYS_B + GT_KEYS)
        phase_A(h)
        if _STOP == "A%d" % h:
            break
        S.fence(["wup0", "wup1", "wup2"], ["hnT%d" % k for k in range(4)] + ["u_sb"])
        S.fence([k for k in ARENA_KEYS_B if not k.startswith("wup")] + GT_KEYS, ARENA_KEYS_A)
        S.fence(WD_KEYS, WA_KEYS)
        phase_B(h)
        if _STOP == "B%d" % h:
            break
        phase_C(h)
    S.dma("sp", "osm", osm_d[:, :], osm[:], r=["osm"] + ["osmS%d" % c for c in range(48)] + ["osmP%d" % c for c in range(48)])
    S.finish("sp")


def _fm(v, ntile):
    return np.ascontiguousarray(np.asarray(v, np.float32).reshape(ntile, 128).T)


_NC_CACHE = {}


def kernel(x_prompt, x_sample, state_lru_h, state_lru_conv, state_ffn_conv,
           g_pre1, w_in, w_conv_a, b_conv_a, w_r, b_r, w_i, b_i, lam, g_out_a,
           g_v, b_v, w_s, b_s, g_out_b, w_o, g_post1,
           g_pre2, w_up, w_conv_f, b_conv_f, w_down, g_post2):
    f = lambda a: np.asarray(a, np.float32)
    x_prompt, x_sample = f(x_prompt), f(x_sample)
    L = 0
    win_r = np.ascontiguousarray(f(w_in)[L].reshape(8, 128, 2048).transpose(1, 0, 2))
    wo_r = np.ascontiguousarray(f(w_o)[L].reshape(8, 128, 1024).transpose(1, 0, 2))
    wu = f(w_up)[L].reshape(8, 128, 2, 24, 128)
    wup_r = np.ascontiguousarray(wu.transpose(3, 1, 0, 2, 4)).reshape(24, 128, 2048)
    wdn_r = np.ascontiguousarray(f(w_down)[L].reshape(24, 128, 1024).transpose(1, 0, 2))
    wsT = np.ascontiguousarray(f(w_s)[L].transpose(0, 2, 1))
    bs_r = np.ascontiguousarray(f(b_s)[L].reshape(1, 4, 128))
    ident = np.eye(128, dtype=np.float32)
    bcv = np.concatenate([f(g_v)[L], f(b_v)[L], f(g_post1)[L], f(g_post2)[L]])
    bc = np.ascontiguousarray(np.broadcast_to(bcv[None, :], (128, KBC)))

    cst_common = np.zeros((128, KCST), np.float32)

    def put(name, arr):
        arr = np.asarray(arr, np.float32).reshape(128, -1)
        cst_common[:, _off[name]:_off[name] + arr.shape[1]] = arr

    put("gpre1", _fm(f(g_pre1)[L], 8))
    put("gpre2", _fm(f(g_pre2)[L], 8))
    put("wca", f(w_conv_a)[L].reshape(4, 4, 128).transpose(2, 1, 0))
    put("bca", _fm(f(b_conv_a)[L], 4))
    put("br", _fm(f(b_r)[L], 4))
    put("bi", _fm(f(b_i)[L], 4))
    put("lam", _fm(f(lam)[L], 4))
    put("goa", _fm(f(g_out_a)[L], 4))
    put("gob", _fm(f(g_out_b)[L], 4))
    put("gv", _fm(f(g_v)[L], 4))
    put("wcf", f(w_conv_f)[L].reshape(3, 48, 128).transpose(2, 1, 0))
    put("bcf", _fm(f(b_conv_f)[L], 48))

    in_maps = []
    for c in range(NCORES):
        cst = cst_common.copy()

        def putc(name, arr):
            arr = np.asarray(arr, np.float32).reshape(128, -1)
            cst[:, _off[name]:_off[name] + arr.shape[1]] = arr

        sh = f(state_lru_h)[L, 2 * c:2 * c + 2]
        putc("h0", sh.reshape(2, 4, 128).transpose(2, 1, 0))
        sc = f(state_lru_conv)[L, 2 * c:2 * c + 2]
        putc("lconv", sc.reshape(2, 3, 4, 128).transpose(3, 2, 0, 1))
        sf = f(state_ffn_conv)[L, 2 * c:2 * c + 2]
        putc("ffnst", sf.reshape(2, 2, 48, 128).transpose(3, 2, 0, 1))
        xt = np.concatenate([x_prompt[c].reshape(16, 128, D),
                             np.concatenate([x_sample[2 * c], x_sample[2 * c + 1]], 0)[None]], 0)
        in_maps.append({
            "x": np.ascontiguousarray(xt), "cst": cst, "bc": bc, "ident": ident,
            "w_in": win_r, "w_o": wo_r, "w_up": wup_r, "w_down": wdn_r,
            "w_r": np.ascontiguousarray(f(w_r)[L]), "w_i": np.ascontiguousarray(f(w_i)[L]),
            "w_sT": wsT, "b_s": bs_r,
        })

    if "nc" not in _NC_CACHE:
        _NC_CACHE["nc"] = build_program()
    nc = _NC_CACHE["nc"]
    res = run_bass_kernel_spmd(nc, in_maps, core_ids=list(range(NCORES)))
    R = res.results

    y_prompt = np.zeros((8, 2048, D), np.float32)
    y_sample = np.zeros((16, 64, D), np.float32)
    hp = np.zeros((1, 8, DA), np.float32)
    cp = np.zeros((1, 8, 3, DA), np.float32)
    fp = np.zeros((1, 8, 2, 2 * DFF), np.float32)
    hs = np.zeros((1, 16, DA), np.float32)
    cs = np.zeros((1, 16, 3, DA), np.float32)
    fs = np.zeros((1, 16, 2, 2 * DFF), np.float32)
    vs = np.zeros((1, 16, 64, 512), np.float32)
    for c in range(NCORES):
        y = np.asarray(R[c]["y"])
        y_prompt[c] = y[:16].reshape(2048, D)
        y_sample[2 * c] = y[16, :64]
        y_sample[2 * c + 1] = y[16, 64:]
        vs[0, 2 * c:2 * c + 2] = np.asarray(R[c]["vns"])
        o = np.asarray(R[c]["osm"])
        oh = o[:, OS_H:OS_H + 12].reshape(128, 4, 3)
        oc = o[:, OS_CONV:OS_CONV + 36].reshape(128, 4, 3, 3)
        of = o[:, OS_FFN:OS_FFN + 288].reshape(128, 48, 3, 2)
        hseq = oh.transpose(2, 1, 0).reshape(3, DA)
        cseq = oc.transpose(2, 3, 1, 0).reshape(3, 3, DA)
        fseq = of.transpose(2, 3, 1, 0).reshape(3, 2, 2 * DFF)
        hp[0, c], cp[0, c], fp[0, c] = hseq[0], cseq[0], fseq[0]
        hs[0, 2 * c:2 * c + 2] = hseq[1:]
        cs[0, 2 * c:2 * c + 2] = cseq[1:]
        fs[0, 2 * c:2 * c + 2] = fseq[1:]
    return (y_prompt, y_sample, hp, cp, fp, hs, cs, fs, vs)
```
